# Optimizing a Trainium2 kernel written in Bass

```python
import jax, jax.numpy as jnp
from jax import lax
import numpy as np

D_MODEL = 1024
BATCH = 1
SEQ = 16384
DEPTH = 2
DEC_BATCH = 32
DEC_SEQ = 8
PAST_LEN = 16384
PAGE_SIZE = 128

HEAD_DIM = 64
D_A = 3 * D_MODEL // 8
N_HEADS_A = D_A // HEAD_DIM
DILATED_PATTERNS = ((128, 1), (512, 4), (2048, 16))
W_MAX = 2048
ROPE_THETA = 10000.0
D_B = D_MODEL // 4
POOL_WINDOWS = (2, 4, 8, 16)
N_POOL_GROUPS = 4
POOL_GROUP = D_B // N_POOL_GROUPS
POOL_MAX = 16
D_C = D_MODEL - D_A - D_B
C_BLOCK = 64
N_BLOCKS_C = D_C // C_BLOCK
CONV_W = 4
RG_C = 8.0
D_IN = 3 * D_A + D_B + 2 * D_C
D_FF = -(-8 * D_MODEL // (3 * 256)) * 256
EPS = 1e-6

kernel_name = "hybrid_dilated_pool_rglru_decoder_step"


def rms_norm(x, g):
    xf = x.astype(jnp.float32)
    y = xf * lax.rsqrt(jnp.mean(xf * xf, axis=-1, keepdims=True) + EPS)
    return (y * g.astype(jnp.float32)).astype(x.dtype)


def rope(x, pos):
    half = HEAD_DIM // 2
    inv = jnp.power(ROPE_THETA, -2.0 * jnp.arange(half, dtype=jnp.float32) / HEAD_DIM)
    ang = pos.astype(jnp.float32)[:, None] * inv[None, :]
    cos = jnp.cos(ang)[None, :, None, :]
    sin = jnp.sin(ang)[None, :, None, :]
    x1, x2 = x[..., :half], x[..., half:]
    return jnp.concatenate([x1 * cos - x2 * sin, x2 * cos + x1 * sin], axis=-1)


def dilated_band_prompt(q, k, v, window, dilation):
    n = window // dilation
    b, s, h, dh = q.shape
    span = n * dilation
    s_pad = -(-s // span) * span
    nb = s_pad // span
    m = s_pad // dilation

    def to_blocks(t):
        t = jnp.pad(t, ((0, 0), (0, s_pad - s), (0, 0), (0, 0))).reshape(b, m, dilation, h, dh)
        return t.transpose(0, 2, 1, 3, 4).reshape(b, dilation, nb, n, h, dh)

    def with_prev(t):
        prev = jnp.pad(t, ((0, 0), (0, 0), (1, 0), (0, 0), (0, 0), (0, 0)))[:, :, :-1]
        return jnp.concatenate([prev, t], axis=3)

    qb = to_blocks(q)
    kk = with_prev(to_blocks(k))
    vv = with_prev(to_blocks(v))
    scores = jnp.einsum("brnqhd,brnkhd->brnhqk", qb, kk) * (1.0 / np.sqrt(HEAD_DIM))
    qi = jnp.arange(n)[:, None]
    ki = jnp.arange(2 * n)[None, :]
    dist = n + qi - ki
    band = (dist >= 0) & (dist <= n)
    has_key = (jnp.arange(nb) > 0)[:, None, None] | (ki >= n)[None]
    mask = band[None] & has_key
    scores = jnp.where(mask[:, None], scores, -jnp.inf)
    lse = jax.nn.logsumexp(scores, axis=-1)
    p = jnp.exp(scores - lse[..., None])
    o = jnp.einsum("brnhqk,brnkhd->brnqhd", p, vv)
    o = o.reshape(b, dilation, m, h, dh).transpose(0, 2, 1, 3, 4).reshape(b, s_pad, h, dh)[:, :s]
    lse = lse.transpose(0, 1, 2, 4, 3).reshape(b, dilation, m, h).transpose(0, 2, 1, 3).reshape(b, s_pad, h)[:, :s]
    return o, lse


def dilated_gather_sample(q, k_all, v_all, window, dilation):
    n = window // dilation
    t = q.shape[1]
    L = k_all.shape[1] - t
    idx = L + jnp.arange(t)[:, None] - dilation * jnp.arange(n + 1)[None, :]
    valid = idx >= 0
    idx = jnp.maximum(idx, 0)
    kg = k_all[:, idx]
    vg = v_all[:, idx]
    scores = jnp.einsum("bthd,btjhd->bthj", q, kg) * (1.0 / np.sqrt(HEAD_DIM))
    scores = jnp.where(valid[None, :, None, :], scores, -jnp.inf)
    lse = jax.nn.logsumexp(scores, axis=-1)
    p = jnp.exp(scores - lse[..., None])
    return jnp.einsum("bthj,btjhd->bthd", p, vg), lse


def dilated_attention(q, k, v, pos, cache_k, cache_v):
    b, t = q.shape[:2]
    qf = rope(q.astype(jnp.float32), pos)
    kf = rope(k.astype(jnp.float32), pos)
    vf = v.astype(jnp.float32)
    outs, lses = [], []
    if cache_k is None:
        for window, dilation in DILATED_PATTERNS:
            o, l = dilated_band_prompt(qf, kf, vf, window, dilation)
            outs.append(o)
            lses.append(l)
        keep = min(W_MAX, t)
        new_k = kf[:, -keep:].astype(k.dtype)
        new_v = vf[:, -keep:].astype(v.dtype)
    else:
        k_all = jnp.concatenate([cache_k.astype(jnp.float32), kf], axis=1)
        v_all = jnp.concatenate([cache_v.astype(jnp.float32), vf], axis=1)
        for window, dilation in DILATED_PATTERNS:
            o, l = dilated_gather_sample(qf, k_all, v_all, window, dilation)
            outs.append(o)
            lses.append(l)
        keep = min(W_MAX, k_all.shape[1])
        new_k = k_all[:, -keep:].astype(cache_k.dtype)
        new_v = v_all[:, -keep:].astype(cache_v.dtype)
    alpha = jax.nn.softmax(jnp.stack(lses, axis=0), axis=0)
    o = jnp.einsum("gbth,gbthd->bthd", alpha, jnp.stack(outs, axis=0))
    return o.reshape(b, t, D_A), new_k, new_v


def pool_mixer(u, pos, buf, w_map, scale):
    b, t, _ = u.shape
    full = u if buf is None else jnp.concatenate([buf.astype(u.dtype), u], axis=1)
    lp = full.shape[1] - t
    cs = jnp.pad(jnp.cumsum(full.astype(jnp.float32), axis=1), ((0, 0), (1, 0), (0, 0)))
    hi_rows = lp + 1 + jnp.arange(t)
    hi = cs[:, lp + 1:]
    means = []
    for g, w in enumerate(POOL_WINDOWS):
        sl = slice(g * POOL_GROUP, (g + 1) * POOL_GROUP)
        lo = jnp.take(cs[..., sl], jnp.maximum(hi_rows - w, 0), axis=1)
        cnt = jnp.minimum(pos + 1, w).astype(jnp.float32)[None, :, None]
        means.append((hi[..., sl] - lo) / cnt)
    pooled = jnp.concatenate(means, axis=-1) - u.astype(jnp.float32)
    y = jnp.einsum("btgc,gcd->btgd", pooled.reshape(b, t, N_POOL_GROUPS, POOL_GROUP), w_map.astype(jnp.float32))
    y = y.reshape(b, t, D_B) * scale.astype(jnp.float32)
    return y, full[:, -(POOL_MAX - 1):]


def rglru_mixer(xr, gate, conv_buf, h0, conv_w, conv_b, gate_a_w, gate_a_b, gate_x_w, gate_x_b, lru_lambda):
    b, t, _ = xr.shape
    pre = jnp.zeros((b, CONV_W - 1, D_C), xr.dtype) if conv_buf is None else conv_buf.astype(xr.dtype)
    full = jnp.concatenate([pre, xr], axis=1)
    xc = conv_b.astype(jnp.float32) + sum(
        full[:, j:j + t].astype(jnp.float32) * conv_w[j].astype(jnp.float32) for j in range(CONV_W))
    xb = xc.reshape(b, t, N_BLOCKS_C, C_BLOCK)
    r = jax.nn.sigmoid(jnp.einsum("btnc,ncd->btnd", xb, gate_a_w.astype(jnp.float32)).reshape(b, t, D_C)
                       + gate_a_b.astype(jnp.float32))
    i = jax.nn.sigmoid(jnp.einsum("btnc,ncd->btnd", xb, gate_x_w.astype(jnp.float32)).reshape(b, t, D_C)
                       + gate_x_b.astype(jnp.float32))
    log_a = RG_C * r * jax.nn.log_sigmoid(lru_lambda.astype(jnp.float32))
    a = jnp.exp(log_a)
    inp = jnp.sqrt(-jnp.expm1(2.0 * log_a)) * i * xc
    if h0 is not None:
        inp = inp.at[:, 0].add(a[:, 0] * h0.astype(jnp.float32))

    def combine(e1, e2):
        a1, b1 = e1
        a2, b2 = e2
        return a1 * a2, a2 * b1 + b2

    _, h = lax.associative_scan(combine, (a, inp), axis=1)
    y = h * jax.nn.gelu(gate.astype(jnp.float32))
    new_h = h[:, -1] if h0 is None else h[:, -1].astype(h0.dtype)
    return y, full[:, -(CONV_W - 1):], new_h


def decoder_layer(x, pos, state, norm1_g, w_in, pool_w, pool_scale, conv_w, conv_b, gate_a_w, gate_a_b,
                  gate_x_w, gate_x_b, lru_lambda, w_out, norm2_g, w_gu, w_down):
    b, t, _ = x.shape
    if state is None:
        ck = cv = pbuf = cbuf = h0 = None
    else:
        ck, cv, pbuf, cbuf, h0 = state
    hn = rms_norm(x, norm1_g)
    proj = hn @ w_in
    cuts = [D_A, 2 * D_A, 3 * D_A, 3 * D_A + D_B, 3 * D_A + D_B + D_C]
    q, k, v, u, xr, gt = jnp.split(proj, cuts, axis=-1)
    hs = (b, t, N_HEADS_A, HEAD_DIM)
    o_a, nk, nv = dilated_attention(q.reshape(hs), k.reshape(hs), v.reshape(hs), pos, ck, cv)
    o_b, npool = pool_mixer(u, pos, pbuf, pool_w, pool_scale)
    o_c, nconv, nh = rglru_mixer(xr, gt, cbuf, h0, conv_w, conv_b, gate_a_w, gate_a_b, gate_x_w, gate_x_b,
                                 lru_lambda)
    mix = jnp.concatenate([o_a, o_b, o_c], axis=-1).astype(x.dtype) @ w_out
    x = x + mix
    h2 = rms_norm(x, norm2_g)
    g_, up = jnp.split(h2 @ w_gu, 2, axis=-1)
    x = x + (jax.nn.silu(g_) * up) @ w_down
    return x, (nk, nv, npool, nconv, nh)


def setup_inputs(seed: int = 0) -> dict:
    key = jax.random.key(seed)
    ks = jax.random.split(key, 24)
    f32 = jnp.float32
    w_buf = min(W_MAX, PAST_LEN)
    nrm = lambda k, shape, s=1.0: jax.random.normal(k, shape, f32) * s
    u_lam = jax.random.uniform(ks[15], (DEPTH, D_C), f32, 0.9, 0.999)
    return {
        "x_prompt": nrm(ks[0], (BATCH, SEQ, D_MODEL)),
        "x_sample": nrm(ks[1], (DEC_BATCH, DEC_SEQ, D_MODEL)),
        "cache_win_k": nrm(ks[2], (DEPTH, DEC_BATCH, w_buf, N_HEADS_A, HEAD_DIM)),
        "cache_win_v": nrm(ks[3], (DEPTH, DEC_BATCH, w_buf, N_HEADS_A, HEAD_DIM)),
        "state_pool": nrm(ks[4], (DEPTH, DEC_BATCH, POOL_MAX - 1, D_B)),
        "state_conv": nrm(ks[5], (DEPTH, DEC_BATCH, CONV_W - 1, D_C)),
        "state_rglru": nrm(ks[6], (DEPTH, DEC_BATCH, D_C), 0.5),
        "norm1_g": 1.0 + nrm(ks[7], (DEPTH, D_MODEL), 0.02),
        "w_in": nrm(ks[8], (DEPTH, D_MODEL, D_IN), D_MODEL ** -0.5),
        "pool_w": nrm(ks[9], (DEPTH, N_POOL_GROUPS, POOL_GROUP, POOL_GROUP), POOL_GROUP ** -0.5),
        "pool_scale": 1.0 + nrm(ks[10], (DEPTH, D_B), 0.1),
        "conv_w": nrm(ks[11], (DEPTH, CONV_W, D_C), CONV_W ** -0.5),
        "conv_b": nrm(ks[12], (DEPTH, D_C), 0.01),
        "gate_a_w": nrm(ks[13], (DEPTH, N_BLOCKS_C, C_BLOCK, C_BLOCK), C_BLOCK ** -0.5),
        "gate_a_b": nrm(ks[14], (DEPTH, D_C), 0.01),
        "gate_x_w": nrm(ks[16], (DEPTH, N_BLOCKS_C, C_BLOCK, C_BLOCK), C_BLOCK ** -0.5),
        "gate_x_b": nrm(ks[17], (DEPTH, D_C), 0.01),
        "lru_lambda": jnp.log(u_lam) - jnp.log1p(-u_lam),
        "w_out": nrm(ks[18], (DEPTH, D_MODEL, D_MODEL), D_MODEL ** -0.5),
        "norm2_g": 1.0 + nrm(ks[19], (DEPTH, D_MODEL), 0.02),
        "w_gu": nrm(ks[20], (DEPTH, D_MODEL, 2 * D_FF), D_MODEL ** -0.5),
        "w_down": nrm(ks[21], (DEPTH, D_FF, D_MODEL), D_FF ** -0.5),
        "final_g": 1.0 + nrm(ks[22], (D_MODEL,), 0.02),
    }


def reference(x_prompt, x_sample, cache_win_k, cache_win_v, state_pool, state_conv, state_rglru,
              norm1_g, w_in, pool_w, pool_scale, conv_w, conv_b, gate_a_w, gate_a_b, gate_x_w, gate_x_b,
              lru_lambda, w_out, norm2_g, w_gu, w_down, final_g):
    pos_p = jnp.arange(x_prompt.shape[1], dtype=jnp.int32)
    pos_s = PAST_LEN + jnp.arange(x_sample.shape[1], dtype=jnp.int32)
    hp, hs = x_prompt, x_sample
    st_p, st_s = [], []
    for l in range(DEPTH):
        lw = (norm1_g[l], w_in[l], pool_w[l], pool_scale[l], conv_w[l], conv_b[l], gate_a_w[l], gate_a_b[l],
              gate_x_w[l], gate_x_b[l], lru_lambda[l], w_out[l], norm2_g[l], w_gu[l], w_down[l])
        hp, sp = decoder_layer(hp, pos_p, None, *lw)
        hs, ss = decoder_layer(hs, pos_s, (cache_win_k[l], cache_win_v[l], state_pool[l], state_conv[l],
                                           state_rglru[l]), *lw)
        st_p.append(sp)
        st_s.append(ss)
    y_prompt = rms_norm(hp, final_g)
    y_sample = rms_norm(hs, final_g)
    p_win_k = jnp.stack([s[0] for s in st_p], axis=0)
    p_win_v = jnp.stack([s[1] for s in st_p], axis=0)
    p_pool = jnp.stack([s[2] for s in st_p], axis=0)
    p_conv = jnp.stack([s[3] for s in st_p], axis=0)
    p_rglru = jnp.stack([s[4] for s in st_p], axis=0)
    s_win_k = jnp.stack([s[0] for s in st_s], axis=0)
    s_win_v = jnp.stack([s[1] for s in st_s], axis=0)
    s_pool = jnp.stack([s[2] for s in st_s], axis=0)
    s_conv = jnp.stack([s[3] for s in st_s], axis=0)
    s_rglru = jnp.stack([s[4] for s in st_s], axis=0)
    return (y_prompt, y_sample, p_win_k, p_win_v, p_pool, p_conv, p_rglru,
            s_win_k, s_win_v, s_pool, s_conv, s_rglru)
```

```python
import contextlib
import numpy as np
import concourse.bass as bass
import concourse.mybir as mybir
from concourse.bass_utils import run_bass_kernel_spmd

F32 = mybir.dt.float32
BF16 = mybir.dt.bfloat16
AF = mybir.ActivationFunctionType
ALU = mybir.AluOpType

SEM_CAP = 30000
NCORE = 8
D = 1024
SEQ = 16384
C = 1024
NCH = SEQ // C
DFF = 2816
NS = 32
EPS = 1e-6


class Op:
    __slots__ = ("eng", "fn", "deps", "inc", "is_dma", "tag", "sem", "val")

    def __init__(self, eng, fn, is_dma=False, tag=None):
        self.eng, self.fn, self.deps, self.inc = eng, fn, [], False
        self.is_dma, self.tag, self.sem, self.val = is_dma, tag, None, None


class Prog:
    ENG = ("pe", "act", "dve", "pool", "sp")

    def __init__(self, nc):
        self.nc = nc
        self.ops = {e: [] for e in self.ENG}
        self.last_w, self.readers, self.all_ops = {}, {}, []

    def _track(self, op, reads, writes):
        deps = []
        for k in reads:
            w = self.last_w.get(k)
            if w is not None:
                deps.append((w, True))
        for k in writes:
            w = self.last_w.get(k)
            if w is not None:
                deps.append((w, False))
            for r in self.readers.get(k, ()):
                deps.append((r, False))
        for k in writes:
            self.last_w[k] = op
            self.readers[k] = []
        for k in reads:
            self.readers.setdefault(k, []).append(op)
        seen = set()
        for d, raw in deps:
            if d is op or id(d) in seen:
                continue
            if (not d.is_dma) and (not op.is_dma) and d.eng == op.eng:
                if op.eng == "pe" or not raw:
                    continue
            seen.add(id(d))
            op.deps.append(d)
            d.inc = True

    defer = None

    def flush(self, lst, n):
        for _ in range(min(n, len(lst))):
            kind, a = lst.pop(0)
            (self.op if kind == "op" else self.dma)(*a[0], **a[1])

    def op(self, eng, fn, reads=(), writes=()):
        if self.defer is not None:
            self.defer.append(("op", ((eng, fn, reads, writes), {})))
            return None
        o = Op(eng, fn)
        self._track(o, reads, writes)
        self.ops[eng].append(o)
        self.all_ops.append(o)
        return o

    def dma(self, q, out, in_, reads=(), writes=(), tag=None, **kw):
        if self.defer is not None:
            self.defer.append(("dma", ((q, out, in_), dict(reads=reads, writes=writes, tag=tag, **kw))))
            return None
        o = Op(q, lambda e: e.dma_start(out=out, in_=in_, **kw), is_dma=True, tag=tag)
        self._track(o, reads, writes)
        o.inc = True
        self.ops[q].append(o)
        self.all_ops.append(o)
        return o

    def barrier(self):
        lasts = []
        for e in self.ENG:
            for x in reversed(self.ops[e]):
                if not x.is_dma and x.fn is not None:
                    lasts.append(x)
                    break
        lastd = {}
        for x in self.all_ops:
            if x.is_dma:
                lastd[x.tag] = x
        deps = lasts + list(lastd.values())
        for d in deps:
            d.inc = True
        for e in self.ENG:
            o = Op(e, None)
            o.deps = [d for d in deps if not (d.eng == e and not d.is_dma and e == "pe")]
            self.ops[e].append(o)
            self.all_ops.append(o)

    def final_wait(self, eng="sp"):
        o = Op(eng, None)
        last = {}
        for x in self.all_ops:
            if x.is_dma:
                last[x.tag] = x
        o.deps = list(last.values())
        self.ops[eng].append(o)
        self.all_ops.append(o)

    def emit(self):
        nc = self.nc
        with contextlib.ExitStack() as st:
            ctr = [0]

            def new_sem(name):
                ctr[0] += 1
                return st.enter_context(nc.semaphore(f"{name}_{ctr[0]}"))

            cur = {e: [new_sem("e_" + e), 0] for e in self.ENG}
            tagsem = {}
            for o in self.all_ops:
                if not o.inc:
                    continue
                if o.is_dma:
                    if o.tag not in tagsem or tagsem[o.tag][1] >= SEM_CAP:
                        tagsem[o.tag] = [new_sem("d"), 0]
                    ts = tagsem[o.tag]
                    ts[1] += 16
                    o.sem, o.val = ts[0], ts[1]
                else:
                    c = cur[o.eng]
                    if c[1] >= SEM_CAP:
                        c = cur[o.eng] = [new_sem("e_" + o.eng), 0]
                    c[1] += 1
                    o.sem, o.val = c[0], c[1]

            def run(engname):
                def body(eng):
                    waited = {}
                    for o in self.ops[engname]:
                        for d in o.deps:
                            key = id(d.sem)
                            if waited.get(key, 0) >= d.val:
                                continue
                            waited[key] = d.val
                            eng.wait_ge(d.sem, d.val)
                        if o.fn is None:
                            continue
                        ins = o.fn(eng)
                        if o.inc:
                            ins.then_inc(o.sem, 16 if o.is_dma else 1)
                return body

            with nc.Block() as block:
                block.tensor(run("pe"))
                block.scalar(run("act"))
                block.vector(run("dve"))
                block.gpsimd(run("pool"))
                block.sync(run("sp"))


def _mult(delta):
    delta = np.asarray(delta)
    m = ((delta >= 0) & (delta <= 128)).astype(np.float32)
    m += ((delta >= 0) & (delta <= 512) & (delta % 4 == 0))
    m += ((delta >= 0) & (delta <= 2048) & (delta % 16 == 0))
    return m.astype(np.float32)


def _rope_tables(pos):
    half = 32
    inv = np.power(np.float32(10000.0), (-2.0 * np.arange(half, dtype=np.float32) / np.float32(64))).astype(np.float32)
    ang = pos.astype(np.float32)[None, :] * inv[:, None]
    cos = np.cos(ang).astype(np.float32)
    sin = np.sin(ang).astype(np.float32)
    cos128 = np.concatenate([cos, cos, cos, cos], 0)
    sin128 = np.concatenate([-sin, sin, -sin, sin], 0)
    return np.ascontiguousarray(cos128), np.ascontiguousarray(sin128)


def _chunkmajor(w, nk):
    K, M = w.shape
    return np.ascontiguousarray(w.reshape(nk, 128, M // 128, 128).transpose(2, 1, 0, 3).reshape(M // 128, 128, nk * 128))


def _vec128(v):
    return np.ascontiguousarray(v.reshape(-1, 128).T)


def _blockdiag(w):
    o = np.zeros((128, 128), np.float32)
    o[:64, :64] = w[0]
    o[64:, 64:] = w[1]
    return o


def build(nch=NCH, do_sample=True):
    nc = bass.Bass("TRN2", target_bir_lowering=False)
    di = lambda n, s, dt=F32: nc.dram_tensor(n, list(s), dt, kind="ExternalInput").ap()
    do = lambda n, s: nc.dram_tensor(n, list(s), F32, kind="ExternalOutput").ap()
    x_p = di("x_p", [SEQ, D]); x_s = di("x_s", [NS, D])
    ck = di("ck", [2, 4, 2048, 384]); cv = di("cv", [2, 4, 2048, 384])
    st_pool = di("st_pool", [2, 4, 15, 256]); st_conv = di("st_conv", [2, 4, 3, 384]); st_h = di("st_h", [2, 4, 384])
    w_in = di("w_in", [2, 20, 128, 1024]); w_v = di("w_v", [2, 128, 8 * 384])
    w_out = di("w_out", [2, 8, 128, 1024]); w_gu = di("w_gu", [2, 44, 128, 1024]); w_dn = di("w_dn", [2, 8, 128, DFF])
    vecs = di("vecs", [128, 128]); bd = di("bd", [2, 8, 128, 128])
    cosT = di("cosT", [NCH, 128, C]); sinT = di("sinT", [NCH, 128, C])
    cosS = di("cosS", [128, NS]); sinS = di("sinS", [128, NS])
    maskP = di("maskP", [128, 17 * 128]); maskS = di("maskS", [128, 17 * 8])
    ident_d = di("ident", [128, 128]); invcnt_d = di("invcnt", [128, 2 * 16])

    y_p = do("y_p", [SEQ, D]); y_s = do("y_s", [NS, D])
    p_k = do("p_k", [2, 2048, 384]); p_v = do("p_v", [2, 2048, 384])
    p_pool = do("p_pool", [2, 15, 256]); p_conv = do("p_conv", [2, 3, 384]); p_h = do("p_h", [2, 384])
    s_k = do("s_k", [2, 4, 2048, 384]); s_v = do("s_v", [2, 4, 2048, 384])
    s_pool = do("s_pool", [2, 4, 15, 256]); s_conv = do("s_conv", [2, 4, 3, 384]); s_h = do("s_h", [2, 4, 384])
    wb_in = nc.dram_tensor("wb_in", [2, 20, 128, 1024], BF16); wb_v = nc.dram_tensor("wb_v", [2, 128, 8 * 384], BF16)
    wb_out = nc.dram_tensor("wb_out", [2, 8, 128, 1024], BF16); wb_gu = nc.dram_tensor("wb_gu", [2, 44, 128, 1024], BF16)
    wb_dn = nc.dram_tensor("wb_dn", [2, 8, 128, DFF], BF16)
    kt_ring = nc.dram_tensor("kt_ring", [2, 3, 128, 3 * C], BF16)
    v1_ring = nc.dram_tensor("v1_ring", [2, 3, 128, 8 * 390], BF16)

    with contextlib.ExitStack() as st:
        st.enter_context(nc.allow_non_contiguous_dma(reason="small state/vec transposes"))
        sb = lambda name, shape, dt=F32: st.enter_context(nc.sbuf_tensor(name, list(shape), dt))
        xT = sb("xT", [128, 8, C]); hnT = sb("hnT", [128, 8, C], BF16)
        mixT = hnT
        ARENA = sb("ARENA", [128, 12800])
        AB = ARENA[:, :].bitcast(BF16)
        kT = AB[:, 0:9216].rearrange("p (k n) -> p k n", k=3)
        V1 = AB[:, 9216:18576].rearrange("p (t h d) -> p t h d", t=24, h=6)
        qT = AB[:, 18576:21648].rearrange("p (k n) -> p k n", k=3)
        g_sb = AB[:, 21648:24720].rearrange("p (k n) -> p k n", k=3)
        hT = AB[:, 0:22528].rearrange("p (k n) -> p k n", k=22)
        yf = ARENA[:, 0:4096].rearrange("p (k n) -> p k n", k=8)
        stage = ARENA[:, 0:6144].rearrange("p (t f) -> p t f", t=16)
        kTc = AB[:, 12288:18432].rearrange("p (k n) -> p k n", k=3)
        V1c = AB[:, 18432:24672].rearrange("p (t h d) -> p t h d", t=16, h=6)
        qTs = AB[:, 24672:24768].rearrange("p (k n) -> p k n", k=3)
        g_s = AB[:, 24768:24864].rearrange("p (k n) -> p k n", k=3)
        u_sb = sb("u_sb", [128, 2, 15 + C]); xr_sb = sb("xr_sb", [128, 3, 3 + C])
        T = [sb(f"T{i}", [128, 15 + C]) for i in range(4)]
        NSLOT = 7
        WR = [sb(f"WR{i}", [128, 1024], BF16) for i in range(NSLOT)]
        Wv = sb("Wv", [128, 8, 384], BF16)
        cos_sb = sb("cos_sb", [128, C]); sin_sb = sb("sin_sb", [128, C])
        sq = sb("sq", [128, 2, 512], BF16); rs = sb("rs", [128, 512])
        tA = sb("tA", [128, 512]); tB = sb("tB", [128, 512])
        es = [sb(f"es{i}", [128, 512], BF16) for i in range(5)]
        mP = sb("mP", [128, 17 * 128], BF16); mS = sb("mS", [128, 17 * 8], BF16)
        ident = sb("ident_sb", [128, 128]); ones_b = sb("ones_b", [128, 128], BF16)
        vec = sb("vec", [128, 128]); BD = sb("BD", [128, 8, 128]); invcnt = sb("invcnt_sb", [128, 2, 16])
        osb = sb("osb", [128, 6, 64]); rden = sb("rden", [128, 6]); den = sb("den", [128, 6])
        ytok = sb("ytok", [128, D]); vst = sb("vst", [128, 384])
        xtok = ytok
        utail = sb("utail", [128, 2, 2, 15]); xrtail = sb("xrtail", [128, 2, 3, 3]); hst = sb("hst", [128, 2, 3])
        V1n = T[0][0:8, 0:780].bitcast(BF16).rearrange("p (t h d) -> p t h d", t=4, h=6)
        u_s = T[0][:, 780:964].rearrange("p (k a t) -> p k a t", k=2, a=4)
        xr_s = T[1][:, 0:132].rearrange("p (k a t) -> p k a t", k=3, a=4)
        TSP = [T[1][:, 132 + 92 * i:224 + 92 * i].rearrange("p (a t) -> p a t", a=4) for i in range(2)]
        TS = [T[1][:, 316 + 32 * i:348 + 32 * i] for i in range(4)]
        kTs = T[1][:, 444:492].bitcast(BF16).rearrange("p (k n) -> p k n", k=3)
        kfs = T[1][:, 492:588].rearrange("p (k n) -> p k n", k=3)
        hs0 = T[1][:, 588:600].rearrange("p (k a) -> p k a", k=3)
        PS = [st.enter_context(nc.psum_tensor(f"ps{i}", [128, 512], F32)) for i in range(8)]

        P = Prog(nc)
        state = {"bank": 0, "slot": 0, "ev": 0, "es": 0}

        def bank():
            b = state["bank"]; state["bank"] = (b + 1) % 6
            return b

        def acc_bank():
            state["acc"] = 13 - state.get("acc", 7)
            return state["acc"]

        def ev_eng():
            state["ev"] ^= 1
            return "act" if state["ev"] else "dve"

        def MM(out, lhsT, rhs, start, stop, reads, writes):
            P.op("pe", lambda e: e.matmul(out, lhsT, rhs, start=start, stop=stop), reads, writes)

        def TR(out, in_, idn, reads, writes):
            P.op("pe", lambda e: e.transpose(out, in_, idn), reads, writes)

        def ACT(out, in_, func, reads, writes, bias=None, scale=None):
            kw = {}
            if bias is not None: kw["bias"] = bias
            if scale is not None: kw["scale"] = scale
            P.op("act", lambda e: e.activation(out, in_, func, **kw), reads, writes)

        def TT(eng, out, a, b, op, reads, writes):
            P.op(eng, lambda e: e.tensor_tensor(out, a, b, op), reads, writes)

        def TSC(eng, out, a, s1, s2, op0, op1, reads, writes):
            if op1 is None:
                P.op(eng, lambda e: e.tensor_scalar(out, a, s1, None, op0), reads, writes)
            else:
                P.op(eng, lambda e: e.tensor_scalar(out, a, s1, s2, op0, op1), reads, writes)

        def STT(eng, out, a, s, b, op0, op1, reads, writes):
            P.op(eng, lambda e: e.scalar_tensor_tensor(out, a, s, b, op0, op1), reads, writes)

        def CP(eng, out, in_, reads, writes):
            if eng == "act":
                P.op("act", lambda e: e.activation(out, in_, AF.Copy), reads, writes)
            else:
                P.op(eng, lambda e: e.tensor_copy(out, in_), reads, writes)

        def RECIP(out, in_, reads, writes):
            P.op("dve", lambda e: e.reciprocal(out, in_), reads, writes)

        def SCAN(out, a, b, init, reads, writes):
            P.op("dve", lambda e: e.tensor_tensor_scan(out, a, b, init, ALU.mult, ALU.add), reads, writes)

        def MEMSET(ap, v, writes):
            P.op("pool", lambda e: e.memset(ap, v), (), writes)

        VC = lambda c: vec[:, c:c + 1]
        INVW = lambda k: VC(98 + k)
        FG = lambda k: VC(90 + k)

        P.dma("sp", vec[:, :], vecs, writes=["vec"], tag="c0")
        P.dma("sp", ident[:, :], ident_d, writes=["ident"], tag="c1")
        P.dma("sp", invcnt[:, :, :].rearrange("p a b -> p (a b)"), invcnt_d, writes=["invcnt"], tag="c2")
        P.dma("pool", mP[:, :], maskP, writes=["mP"], tag="c3")
        P.dma("pool", mS[:, :], maskS, writes=["mS"], tag="c4")
        MEMSET(ones_b[:, :], 1.0, ["ones"])
        MEMSET(utail[:, :, :, :].rearrange("p a b c -> p (a b c)"), 0.0, ["utail"])
        MEMSET(xrtail[:, :, :, :].rearrange("p a b c -> p (a b c)"), 0.0, ["xrtail"])
        MEMSET(hst[:, :, :].rearrange("p a b -> p (a b)"), 0.0, ["hst"])
        for l in range(2):
            lam = vec[:, 45 * l + 39:45 * l + 42]
            ACT(lam, lam, AF.Exp, ["vec"], ["vec"], scale=-1.0)
            ACT(lam, lam, AF.Ln, ["vec"], ["vec"], bias=1.0, scale=1.0)
            TSC("dve", vec[:, 45 * l + 42:45 * l + 45], lam, -16.0, None, ALU.mult, None, ["vec"], ["vec"])
            TSC("dve", lam, lam, -8.0, None, ALU.mult, None, ["vec"], ["vec"])
        if do_sample:
            for l in range(2):
                for b4 in range(4):
                    P.dma("sp", s_k[l, b4, 0:2040, :].rearrange("(a r) f -> a (r f)", a=120), ck[l, b4, 8:2048, :].rearrange("(a r) f -> a (r f)", a=120), tag="sk")
                    P.dma("sp", s_v[l, b4, 0:2040, :].rearrange("(a r) f -> a (r f)", a=120), cv[l, b4, 8:2048, :].rearrange("(a r) f -> a (r f)", a=120), tag="sv")

        wplan = []
        AHEAD = 4

        def plan_layer(l):
            for j in range(3):
                for base in (0, 3):
                    wplan.append((wb_in.ap()[l, base + j], 1024)); wplan.append((wb_in.ap()[l, base + 6 + j], 1024))
            for c in (12, 13, 14, 15, 16, 17, 18, 19):
                wplan.append((wb_in.ap()[l, c], 1024))
            for m in range(8):
                wplan.append((wb_out.ap()[l, m], 1024))
            for m in range(22):
                wplan.append((wb_gu.ap()[l, m], 1024)); wplan.append((wb_gu.ap()[l, 22 + m], 1024))
            for m in range(8):
                wplan.append((wb_dn.ap()[l, m, :, 0:1024], 1024)); wplan.append((wb_dn.ap()[l, m, :, 1024:2048], 1024))
                wplan.append((wb_dn.ap()[l, m, :, 2048:DFF], DFF - 2048))

        def load_w(dram_ap, ncols=1024):
            i = state.get("wc", 0); state["wc"] = i + 1
            while state.get("wi", 0) < min(len(wplan), i + 1 + AHEAD):
                k = state.get("wi", 0)
                ap, nc_ = wplan[k]
                P.dma("sp", WR[k % NSLOT][:, 0:nc_], ap, reads=["wb"], writes=[("w", k % NSLOT)], tag=f"w{k % NSLOT}")
                state["wi"] = k + 1
            assert wplan[i][1] == ncols, (i, wplan[i][1], ncols)
            return i % NSLOT

        def convert_weights():
            def conv(dst, src, ncols):
                s = state["slot"]; state["slot"] = (s + 1) % NSLOT
                P.dma("pool", WR[s][:, 0:ncols], src, writes=[("w", s)], tag=f"w{s}")
                P.dma("sp", dst, WR[s][:, 0:ncols], reads=[("w", s)], writes=["wb"], tag="wbs")
            for l in range(2):
                for m in range(20):
                    conv(wb_in.ap()[l, m], w_in[l, m], 1024)
                for i in range(3):
                    conv(wb_v.ap()[l, :, i * 1024:(i + 1) * 1024], w_v[l, :, i * 1024:(i + 1) * 1024], 1024)
                for m in range(8):
                    conv(wb_out.ap()[l, m], w_out[l, m], 1024)
                for m in range(44):
                    conv(wb_gu.ap()[l, m], w_gu[l, m], 1024)
                for m in range(8):
                    for (a0, a1) in ((0, 1024), (1024, 2048), (2048, DFF)):
                        conv(wb_dn.ap()[l, m, :, a0:a1], w_dn[l, m, :, a0:a1], a1 - a0)

        def mm(slots, nk, src, skey, c0, n):
            b = bank()
            kc = 0
            for (s, cnt) in slots:
                for i in range(cnt):
                    MM(PS[b][:, 0:n], WR[s][:, i * 128:(i + 1) * 128], src[:, kc, c0:c0 + n], kc == 0, kc == nk - 1,
                       [("w", s)] + (list(skey) if isinstance(skey, (list, tuple)) else [skey]), [("ps", b)])
                    kc += 1
            return b

        def load_x(dram_rows, ntok, c0):
            P.dma("sp", xtok[0:ntok, :], dram_rows, writes=["ytok"], tag="xtok")
            for half in range(2):
                b = bank()
                for i in range(4):
                    k = half * 4 + i
                    TR(PS[b][:, i * 128:i * 128 + ntok], xtok[0:ntok, k * 128:(k + 1) * 128], ident[0:ntok, 0:ntok], ["ytok", "ident"], [("ps", b)])
                CP(ev_eng(), xT[:, half * 4:half * 4 + 4, c0:c0 + ntok], PS[b][:, :].rearrange("p (i n) -> p i n", i=4)[:, :, 0:ntok], [], [("ps", b), "xT"])

        def rmsnorm(gfn, groups, dst, dkey):
            for (c0, n) in groups:
                b = bank()
                for kc in range(8):
                    ACT(sq[:, kc % 2, 0:n], xT[:, kc, c0:c0 + n], AF.Square, ["xT"], [("sq", kc % 2)])
                    MM(PS[b][:, 0:n], ones_b[:, :], sq[:, kc % 2, 0:n], kc == 0, kc == 7, [("sq", kc % 2), "ones"], [("ps", b)])
                ACT(rs[:, 0:n], PS[b][:, 0:n], AF.Sqrt, [], [("ps", b), "rs"], bias=EPS, scale=1.0 / D)
                RECIP(rs[:, 0:n], rs[:, 0:n], ["rs"], ["rs"])
                for kc in range(8):
                    STT("dve", dst(kc, c0, n), xT[:, kc, c0:c0 + n], gfn(kc), rs[:, 0:n], ALU.mult, ALU.mult, ["xT", "rs", "vec"], [dkey])

        def dense_tail(l, groups):
            for m in range(8):
                s = load_w(wb_out.ap()[l, m])
                for (c0, n) in groups:
                    b = mm([(s, 8)], 8, mixT, ["hnT", "mixA", "mixP", "mixR"], c0, n)
                    TT("dve", xT[:, m, c0:c0 + n], xT[:, m, c0:c0 + n], PS[b][:, 0:n], ALU.add, [], [("ps", b), "xT"])
            rmsnorm(lambda kc: VC(45 * l + 8 + kc), groups, lambda kc, c0, n: hnT[:, kc, c0:c0 + n], "hnT")
            P.barrier()
            for m in range(22):
                sg = load_w(wb_gu.ap()[l, m]); su = load_w(wb_gu.ap()[l, 22 + m])
                for (c0, n) in groups:
                    bg = mm([(sg, 8)], 8, hnT, "hnT", c0, n)
                    bu = mm([(su, 8)], 8, hnT, "hnT", c0, n)
                    ACT(tA[:, 0:n], PS[bg][:, 0:n], AF.Silu, [], [("ps", bg), "tA"])
                    TT("dve", hT[:, m, c0:c0 + n], tA[:, 0:n], PS[bu][:, 0:n], ALU.mult, ["tA"], [("ps", bu), ("hT", m)])
            for m in range(8):
                ss = [(load_w(wb_dn.ap()[l, m, :, 0:1024]), 8), (load_w(wb_dn.ap()[l, m, :, 1024:2048]), 8), (load_w(wb_dn.ap()[l, m, :, 2048:DFF], DFF - 2048), 6)]
                for (c0, n) in groups:
                    b = bank()
                    kc = 0
                    for (s, cnt) in ss:
                        for i in range(cnt):
                            MM(PS[b][:, 0:n], WR[s][:, i * 128:(i + 1) * 128], hT[:, kc, c0:c0 + n], kc == 0, kc == 21, [("w", s), ("hT", kc)], [("ps", b)])
                            kc += 1
                    TT("dve", xT[:, m, c0:c0 + n], xT[:, m, c0:c0 + n], PS[b][:, 0:n], ALU.add, [], [("ps", b), "xT"])
            P.barrier()

        def final_out(groups, out_rows_fn):
            for (c0, n) in groups:
                rmsnorm(FG, [(c0, n)], lambda kc, c0_, n_: yf[:, kc, 0:n_], "yf")
                for t0 in range(0, n, 128):
                    nt = min(128, n - t0)
                    for half in range(2):
                        b = bank()
                        for i in range(4):
                            k = half * 4 + i
                            TR(PS[b][0:nt, i * 128:(i + 1) * 128], yf[:, k, t0:t0 + nt], ident[:, :], ["yf", "ident"], [("ps", b)])
                        CP(ev_eng(), ytok[0:nt, half * 512:(half + 1) * 512], PS[b][0:nt, :], [], [("ps", b), "ytok"])
                    P.dma("sp", out_rows_fn(c0 + t0, nt), ytok[0:nt, :], reads=["ytok"], tag="yout")
            P.barrier()

        def w_in_feature(l, groups, cos_d, sin_d, qdst, kdst, kcol0, udst, xrdst, pview, kf32_hook, gdst):
            for (c0, n) in groups:
                pass
            for j in range(3):
                for (dst, base, col0, isk) in ((qdst, 0, 0, False), (kdst, 3, kcol0, True)):
                    s1 = load_w(wb_in.ap()[l, base + j]); s2 = load_w(wb_in.ap()[l, base + 6 + j])
                    for (c0, n) in groups:
                        b1 = mm([(s1, 8)], 8, hnT, "hnT", c0, n)
                        b2 = mm([(s2, 8)], 8, hnT, "hnT", c0, n)
                        ta, tb, ka, kb = tA, tB, "tA", "tB"
                        TT("dve", ta[:, 0:n], PS[b1][:, 0:n], cos_sb[:, c0:c0 + n], ALU.mult, ["cos"], [("ps", b1), ka])
                        TT("dve", tb[:, 0:n], PS[b2][:, 0:n], sin_sb[:, c0:c0 + n], ALU.mult, ["sin"], [("ps", b2), kb])
                        TT("pool", ta[:, 0:n], ta[:, 0:n], tb[:, 0:n], ALU.add, [kb], [ka])
                        CP("act", dst[:, j, col0 + c0:col0 + c0 + n], ta[:, 0:n], [ka], ["kT" if isk else "qT"])
                        if isk and kf32_hook is not None:
                            kf32_hook(j, c0, n, ta, tb, ka, kb)
            for kc in range(2):
                s = load_w(wb_in.ap()[l, 12 + kc])
                for (c0, n) in groups:
                    b = mm([(s, 8)], 8, hnT, "hnT", c0, n)
                    CP("act", udst(kc, c0, n), pview(PS[b][:, 0:n]), [], [("ps", b), "u"])
            for kc in range(3):
                s = load_w(wb_in.ap()[l, 14 + kc])
                for (c0, n) in groups:
                    b = mm([(s, 8)], 8, hnT, "hnT", c0, n)
                    CP("act", xrdst(kc, c0, n), pview(PS[b][:, 0:n]), [], [("ps", b), "xr"])
            for kc in range(3):
                s = load_w(wb_in.ap()[l, 17 + kc])
                for (c0, n) in groups:
                    b = mm([(s, 8)], 8, hnT, "hnT", c0, n)
                    ACT(gdst[:, kc, c0:c0 + n], PS[b][:, 0:n], AF.Gelu, [], [("ps", b), "g"])

        def load_layer_consts(l):
            P.dma("sp", Wv[:, :, :].rearrange("p k f -> p (k f)"), wb_v.ap()[l], reads=["wb"], writes=["Wv"], tag="wv")
            P.dma("sp", BD[:, :, :], bd[l].rearrange("a p f -> p a f"), writes=["BD"], tag="bd")

        def rglru_core(l, kc, xr0, xr1, xr2, xr3, tmp, mmviews, scan_fn):
            xc, a_, b_, w_ = tmp
            cw = lambda j: VC(45 * l + 18 + 3 * j + kc)
            TSC("pool", xc, xr0, cw(3), VC(45 * l + 30 + kc), ALU.mult, ALU.add, ["xr", "vec"], ["rg0"])
            for j, sh in ((2, xr1), (1, xr2), (0, xr3)):
                STT("dve", xc, sh, cw(j), xc, ALU.mult, ALU.add, ["xr", "vec", "rg0"], ["rg0"])
            for (view, n) in mmviews:
                ba = bank()
                MM(PS[ba][:, 0:n], BD[:, 2 + kc, :], view(xc), True, True, ["rg0", "BD"], [("ps", ba)])
                bx = bank()
                MM(PS[bx][:, 0:n], BD[:, 5 + kc, :], view(xc), True, True, ["rg0", "BD"], [("ps", bx)])
                ACT(view(w_), PS[ba][:, 0:n], AF.Sigmoid, ["vec"], [("ps", ba), "rg3"], bias=VC(45 * l + 33 + kc))
                ACT(view(b_), PS[bx][:, 0:n], AF.Sigmoid, ["vec"], [("ps", bx), "rg2"], bias=VC(45 * l + 36 + kc))
            ACT(a_, w_, AF.Exp, ["rg3", "vec"], ["rg1"], scale=VC(45 * l + 39 + kc))
            ACT(w_, w_, AF.Exp, ["rg3", "vec"], ["rg3"], scale=VC(45 * l + 42 + kc))
            TSC("dve", w_, w_, -1.0, 1.0, ALU.mult, ALU.add, ["rg3"], ["rg3"])
            P.op("dve", lambda e: e.tensor_scalar_max(w_, w_, 0.0), ["rg3"], ["rg3"])
            ACT(w_, w_, AF.Sqrt, ["rg3"], ["rg3"])
            TT("pool", b_, b_, w_, ALU.mult, ["rg2", "rg3"], ["rg2"])
            TT("pool", b_, b_, xc, ALU.mult, ["rg2", "rg0"], ["rg2"])
            scan_fn(a_, b_, w_)

        def prompt_chunk(j):
            G = [(0, 512), (512, 512)]
            for t in range(8):
                load_x(x_p[j * C + t * 128:j * C + (t + 1) * 128, :], 128, t * 128)
            P.dma("sp", cos_sb[:, :], cosT[j], writes=["cos"], tag="cos")
            P.dma("sp", sin_sb[:, :], sinT[j], writes=["sin"], tag="sin")
            for l in range(2):
                load_layer_consts(l)
                MEMSET(V1[:, :, :, 64:65], 1.0, ["V1"])
                for back in (2, 1):
                    if j - back >= 0:
                        sl = (j - back) % 3
                        P.dma("sp", kT[:, :, (2 - back) * C:(3 - back) * C], kt_ring.ap()[l, sl].rearrange("p (k n) -> p k n", k=3),
                              reads=[("ktr", l, sl)], writes=["kT"], tag="ktl")
                        P.dma("sp", V1[:, (2 - back) * 8:(3 - back) * 8, :, :], v1_ring.ap()[l, sl].rearrange("p (t h d) -> p t h d", t=8, h=6),
                              reads=[("v1r", l, sl)], writes=["V1"], tag="v1l")
                rmsnorm(lambda kc: VC(45 * l + kc), G, lambda kc, c0, n: hnT[:, kc, c0:c0 + n], "hnT")
                want = (j >= nch - 2)

                def kf32_hook(jc, c0, n, ta, tb, ka, kb, l=l):
                    if not want:
                        return
                    b = bank()
                    for i in range(n // 128):
                        TR(PS[b][:, i * 128:(i + 1) * 128], ta[:, i * 128:(i + 1) * 128], ident[:, :], [ka, "ident"], [("ps", b)])
                    CP("dve", tb[:, 0:n], PS[b][:, 0:n], [], [("ps", b), kb])
                    r0 = (j - (nch - 2)) * C + c0
                    P.dma("sp", p_k[l, r0:r0 + n, jc * 128:(jc + 1) * 128].rearrange("(i p) f -> p i f", p=128),
                          tb[:, 0:n].rearrange("p (i f) -> p i f", f=128), reads=[kb], tag="pk")

                w_in_feature(l, G, cosT[j], sinT[j], qT, kT, 2 * C,
                             lambda kc, c0, n: u_sb[:, kc, 15 + c0:15 + c0 + n],
                             lambda kc, c0, n: xr_sb[:, kc, 3 + c0:3 + c0 + n], lambda ap: ap, kf32_hook, g_sb)
                for t in range(8):
                    b = bank()
                    for kc in range(8):
                        MM(PS[b][:, 0:384], hnT[:, kc, t * 128:(t + 1) * 128], Wv[:, kc, :], kc == 0, kc == 7, ["hnT", "Wv"], [("ps", b)])
                    if want:
                        CP("act", vst[:, :], PS[b][:, 0:384], [], [("ps", b), "vst"])
                        r0 = (j - (nch - 2)) * C + t * 128
                        P.dma("sp", p_v[l, r0:r0 + 128, :], vst[:, :], reads=["vst"], tag="pv")
                    CP("dve", V1[:, 16 + t, :, 0:64], PS[b][:, 0:384].rearrange("p (h d) -> p h d", h=6), [], [("ps", b), "V1"])
                sl = j % 3
                P.dma("sp", kt_ring.ap()[l, sl].rearrange("p (k n) -> p k n", k=3), kT[:, :, 2 * C:3 * C], reads=["kT"], writes=[("ktr", l, sl)], tag="kts")
                P.dma("sp", v1_ring.ap()[l, sl].rearrange("p (t h d) -> p t h d", t=8, h=6), V1[:, 16:24, :, :], reads=["V1"], writes=[("v1r", l, sl)], tag="v1s")

                P.barrier()
                P.defer = []
                L = 15 + C
                CP("pool", u_sb[:, :, 0:15], utail[:, l, :, :], ["utail"], ["u"])
                for kc in range(2):
                    A, B, Pd = T[0], T[1], T[2]
                    uk = u_sb[:, kc, :]
                    TT("pool", A[:, 1:L], uk[:, 1:L], uk[:, 0:L - 1], ALU.add, ["u"], ["rg0"])
                    if kc == 0:
                        TT("pool", B[64:128, 3:L], A[64:128, 3:L], A[64:128, 1:L - 2], ALU.add, ["rg0"], ["rg1"])
                        CP("pool", B[0:64, 15:L], A[0:64, 15:L], ["rg0"], ["rg1"])
                    else:
                        TT("pool", B[:, 3:L], A[:, 3:L], A[:, 1:L - 2], ALU.add, ["rg0"], ["rg1"])
                        TT("pool", A[:, 7:L], B[:, 7:L], B[:, 3:L - 4], ALU.add, ["rg1"], ["rg0"])
                        TT("pool", B[64:128, 15:L], A[64:128, 15:L], A[64:128, 7:L - 8], ALU.add, ["rg0"], ["rg1"])
                        CP("pool", B[0:64, 15:L], A[0:64, 15:L], ["rg0"], ["rg1"])
                    STT("dve", Pd[:, 15:L], B[:, 15:L], INVW(kc), uk[:, 15:L], ALU.mult, ALU.subtract, ["rg1", "u", "vec"], ["rg2"])
                    if j == 0:
                        TT("dve", Pd[:, 15:31], B[:, 15:31], invcnt[:, kc, :], ALU.mult, ["rg1", "invcnt"], ["rg2"])
                        TT("dve", Pd[:, 15:31], Pd[:, 15:31], uk[:, 15:31], ALU.subtract, ["u", "rg2"], ["rg2"])
                    for (c0, n) in G:
                        b = bank()
                        MM(PS[b][:, 0:n], BD[:, kc, :], Pd[:, 15 + c0:15 + c0 + n], True, True, ["rg2", "BD"], [("ps", b)])
                        TSC("dve", mixT[:, 3 + kc, c0:c0 + n], PS[b][:, 0:n], VC(45 * l + 16 + kc), None, ALU.mult, None, ["vec"], [("ps", b), "mixP"])
                CP("pool", utail[:, l, :, :], u_sb[:, :, C:C + 15], ["u"], ["utail"])

                CP("pool", xr_sb[:, :, 0:3], xrtail[:, l, :, :], ["xrtail"], ["xr"])
                for kc in range(3):
                    tmp = tuple(T[i][:, 0:C] for i in range(4))
                    xk = xr_sb[:, kc, :]

                    def scan_fn(a_, b_, h_, l=l, kc=kc):
                        SCAN(h_, a_, b_, hst[:, l, kc:kc + 1], ["rg1", "rg2", "hst", "rg3"], ["rg3"])
                        CP("dve", hst[:, l, kc:kc + 1], h_[:, C - 1:C], ["rg3"], ["hst"])
                        TT("dve", mixT[:, 5 + kc, :], h_, g_sb[:, kc, :], ALU.mult, ["rg3", "g"], ["mixR"])

                    mmv = [((lambda ap, c0=c0, n=n: ap[:, c0:c0 + n]), n) for (c0, n) in G]
                    rglru_core(l, kc, xk[:, 3:3 + C], xk[:, 2:2 + C], xk[:, 1:1 + C], xk[:, 0:C], tmp, mmv, scan_fn)
                CP("pool", xrtail[:, l, :, :], xr_sb[:, :, C:C + 3], ["xr"], ["xrtail"])
                side = P.defer; P.defer = None
                jobs = []
                for tq in range(8):
                    Tg = 8 * j + tq
                    offs = [o for o in range(17) if Tg - o >= 0]
                    bacc = acc_bank()
                    batches = [offs[i:i + 4] for i in range(0, len(offs), 4)]
                    for h in range(6):
                        for bi_, ob in enumerate(batches):
                            jobs.append((tq, h, ob, bi_ == 0, bi_ == len(batches) - 1, bacc, h == 5 and bi_ == len(batches) - 1))
                LAG = 3
                P.flush(side, len(side))
                per = -(-len(side) // max(1, len(jobs)))
                inflight = []
                for idx in range(len(jobs) + LAG):
                    if idx < len(jobs):
                        tq, h, ob, first, last, bacc, endq = jobs[idx]
                        jc, hp = h // 2, 64 * (h % 2)
                        b = bank()
                        nb = len(ob)
                        for i, o in enumerate(ob):
                            kt_ = 16 + tq - o
                            MM(PS[b][:, i * 128:(i + 1) * 128], kT[hp:hp + 64, jc, kt_ * 128:(kt_ + 1) * 128],
                               qT[hp:hp + 64, jc, tq * 128:(tq + 1) * 128], True, True, ["kT", "qT"], [("ps", b)])
                        ei = state["es"]; state["es"] = (ei + 1) % 5
                        ACT(es[ei][:, 0:nb * 128], PS[b][:, 0:nb * 128], AF.Exp, [], [("ps", b), ("es", ei)], scale=0.125)
                        o0 = ob[0]
                        state["mk"] = state.get("mk", 0) ^ 1
                        TT("pool" if state["mk"] else "dve", es[ei][:, 0:nb * 128], es[ei][:, 0:nb * 128], mP[:, o0 * 128:(o0 + nb) * 128], ALU.mult, ["mP"], [("es", ei)])
                        inflight.append((jobs[idx], ei))
                        P.flush(side, per)
                    if idx >= LAG:
                        (tq, h, ob, first, last, bacc, endq), ei = inflight.pop(0)
                        for i, o in enumerate(ob):
                            kt_ = 16 + tq - o
                            MM(PS[bacc][:, h * 65:(h + 1) * 65], es[ei][:, i * 128:(i + 1) * 128], V1[:, kt_, h, :],
                               first and i == 0, last and i == len(ob) - 1, [("es", ei), "V1"], [("ps", bacc)])
                        if endq:
                            accv = PS[bacc][:, 0:390].rearrange("p (h d) -> p h d", h=6)
                            CP("dve", den[:, :], accv[:, :, 64], [], [("ps", bacc), "den"])
                            RECIP(rden[:, :], den[:, :], ["den"], ["rden"])
                            for hh in range(6):
                                TSC("dve", osb[:, hh, :], accv[:, hh, 0:64], rden[:, hh:hh + 1], None, ALU.mult, None, ["rden"], [("ps", bacc), "osb"])
                            bt = bank()
                            for jc2 in range(3):
                                TR(PS[bt][:, jc2 * 128:(jc2 + 1) * 128], osb[:, 2 * jc2:2 * jc2 + 2, :].rearrange("p h d -> p (h d)"), ident[:, :], ["osb", "ident"], [("ps", bt)])
                            CP("act", mixT[:, 0:3, tq * 128:(tq + 1) * 128], PS[bt][:, 0:384].rearrange("p (k n) -> p k n", k=3), [], [("ps", bt), "mixA"])
                P.flush(side, len(side))
                dense_tail(l, G)
            final_out(G, lambda r0, nt: y_p[j * C + r0:j * C + r0 + nt, :])

        def sample_pass():
            G = [(0, NS)]
            v4 = lambda ap: ap.rearrange("p (a t) -> p a t", a=4)
            MEMSET(V1n[:, :, :, 64:65], 1.0, ["V1n"])
            load_x(x_s[:, :], NS, 0)
            P.dma("sp", cos_sb[:, 0:NS], cosS, writes=["cos"], tag="cos")
            P.dma("sp", sin_sb[:, 0:NS], sinS, writes=["sin"], tag="sin")
            for l in range(2):
                load_layer_consts(l)
                rmsnorm(lambda kc: VC(45 * l + kc), G, lambda kc, c0, n: hnT[:, kc, c0:c0 + n], "hnT")
                for b4 in range(4):
                    for k in range(2):
                        P.dma("sp", u_s[:, k, b4, 0:15], st_pool[l, b4, :, k * 128:(k + 1) * 128].rearrange("r p -> p r"), writes=["u"], tag="su")
                    for k in range(3):
                        P.dma("sp", xr_s[:, k, b4, 0:3], st_conv[l, b4, :, k * 128:(k + 1) * 128].rearrange("r p -> p r"), writes=["xr"], tag="sx")
                for k in range(3):
                    P.dma("sp", hs0[:, k, :], st_h[l, :, k * 128:(k + 1) * 128].rearrange("b p -> p b"), writes=["hs0"], tag="sh")

                def kf32_hook(jc, c0, n, ta, tb, ka, kb):
                    CP("dve", kfs[:, jc, :], ta[:, 0:NS], [ka], ["kfs"])

                w_in_feature(l, G, cosS, sinS, qTs, kTs, 0,
                             lambda kc, c0, n: u_s[:, kc, :, 15:23],
                             lambda kc, c0, n: xr_s[:, kc, :, 3:11], v4, kf32_hook, g_s)
                b = bank()
                for jc in range(3):
                    TR(PS[b][0:NS, jc * 128:(jc + 1) * 128], kfs[:, jc, :], ident[:, :], ["kfs", "ident"], [("ps", b)])
                CP("dve", vst[0:NS, :], PS[b][0:NS, 0:384], [], [("ps", b), "vst"])
                for b4 in range(4):
                    P.dma("sp", s_k[l, b4, 2040:2048, :], vst[b4 * 8:(b4 + 1) * 8, :], reads=["vst"], tag="skn")
                b = bank()
                for kc in range(8):
                    MM(PS[b][0:NS, 0:384], hnT[:, kc, 0:NS], Wv[:, kc, :], kc == 0, kc == 7, ["hnT", "Wv"], [("ps", b)])
                CP("act", ytok[0:NS, 0:384], PS[b][0:NS, 0:384], [], [("ps", b), "ytok"])
                for b4 in range(4):
                    P.dma("sp", s_v[l, b4, 2040:2048, :], ytok[b4 * 8:(b4 + 1) * 8, 0:384], reads=["ytok"], tag="svn")
                for b4 in range(4):
                    b = bank()
                    for kc in range(8):
                        MM(PS[b][0:8, 0:384], hnT[:, kc, b4 * 8:(b4 + 1) * 8], Wv[:, kc, :], kc == 0, kc == 7, ["hnT", "Wv"], [("ps", b)])
                    CP("dve", V1n[0:8, b4, :, 0:64], PS[b][0:8, 0:384].rearrange("p (h d) -> p h d", h=6), [], [("ps", b), "V1n"])
                MEMSET(V1c[:, :, :, 64:65], 1.0, ["V1c"])
                for b4 in range(4):
                    P.dma("sp", stage, ck[l, b4].rearrange("(t p) f -> p t f", p=128), writes=["stage"], tag="kc")
                    for t in range(16):
                        b = bank()
                        for jc in range(3):
                            TR(PS[b][:, jc * 128:(jc + 1) * 128], stage[:, t, jc * 128:(jc + 1) * 128], ident[:, :], ["stage", "ident"], [("ps", b)])
                        CP(ev_eng(), kTc[:, :, t * 128:(t + 1) * 128], PS[b][:, 0:384].rearrange("p (k n) -> p k n", k=3), [], [("ps", b), "kTc"])
                    P.dma("sp", stage, cv[l, b4].rearrange("(t p) f -> p t f", p=128), writes=["stage"], tag="kc")
                    CP("pool", V1c[:, :, :, 0:64], stage.rearrange("p t (h d) -> p t h d", h=6), ["stage"], ["V1c"])
                    bacc = acc_bank()
                    qcols = slice(b4 * 8, (b4 + 1) * 8)
                    for h in range(6):
                        jc, hp = h // 2, 64 * (h % 2)
                        b = bank()
                        for t in range(16):
                            MM(PS[b][:, t * 8:(t + 1) * 8], kTc[hp:hp + 64, jc, t * 128:(t + 1) * 128], qTs[hp:hp + 64, jc, qcols], True, True, ["kTc", "qT"], [("ps", b)])
                        MM(PS[b][0:8, 128:136], kTs[hp:hp + 64, jc, qcols], qTs[hp:hp + 64, jc, qcols], True, True, ["kT", "qT"], [("ps", b)])
                        ei = state["es"]; state["es"] = (ei + 1) % 5
                        ACT(es[ei][:, 0:128], PS[b][:, 0:128], AF.Exp, [], [("ps", b), ("es", ei)], scale=0.125)
                        ACT(es[ei][0:8, 128:136], PS[b][0:8, 128:136], AF.Exp, [], [("ps", b), ("es", ei)], scale=0.125)
                        TT("pool", es[ei][:, 0:128], es[ei][:, 0:128], mS[:, 0:128], ALU.mult, ["mS"], [("es", ei)])
                        TT("pool", es[ei][0:8, 128:136], es[ei][0:8, 128:136], mS[0:8, 128:136], ALU.mult, ["mS"], [("es", ei)])
                        for t in range(16):
                            MM(PS[bacc][0:8, h * 65:(h + 1) * 65], es[ei][:, t * 8:(t + 1) * 8], V1c[:, t, h, :], t == 0, False, [("es", ei), "V1c"], [("ps", bacc)])
                        MM(PS[bacc][0:8, h * 65:(h + 1) * 65], es[ei][0:8, 128:136], V1n[0:8, b4, h, :], False, True, [("es", ei), "V1n"], [("ps", bacc)])
                    accv = PS[bacc][0:8, 0:390].rearrange("p (h d) -> p h d", h=6)
                    CP("dve", den[0:8, :], accv[:, :, 64], [], [("ps", bacc), "den"])
                    RECIP(rden[0:8, :], den[0:8, :], ["den"], ["rden"])
                    for h in range(6):
                        TSC("dve", osb[0:8, h, :], accv[:, h, 0:64], rden[0:8, h:h + 1], None, ALU.mult, None, ["rden"], [("ps", bacc), "osb"])
                    bt = bank()
                    for jc in range(3):
                        TR(PS[bt][:, jc * 8:(jc + 1) * 8], osb[0:8, 2 * jc:2 * jc + 2, :].rearrange("p h d -> p (h d)"), ident[0:8, 0:8], ["osb", "ident"], [("ps", bt)])
                    CP("act", mixT[:, 0:3, qcols], PS[bt][:, 0:24].rearrange("p (k n) -> p k n", k=3), [], [("ps", bt), "hnT"])
                for kc in range(2):
                    A, B = TSP[0], TSP[1]
                    Pd = TS[0]
                    uk = u_s[:, kc, :, :]
                    TT("pool", A[:, :, 1:23], uk[:, :, 1:23], uk[:, :, 0:22], ALU.add, ["u"], ["pA"])
                    if kc == 0:
                        TT("pool", B[64:128, :, 3:23], A[64:128, :, 3:23], A[64:128, :, 1:21], ALU.add, ["pA"], ["pB"])
                        CP("pool", B[0:64, :, 15:23], A[0:64, :, 15:23], ["pA"], ["pB"])
                    else:
                        TT("pool", B[:, :, 3:23], A[:, :, 3:23], A[:, :, 1:21], ALU.add, ["pA"], ["pB"])
                        TT("pool", A[:, :, 7:23], B[:, :, 7:23], B[:, :, 3:19], ALU.add, ["pB"], ["pA"])
                        TT("pool", B[64:128, :, 15:23], A[64:128, :, 15:23], A[64:128, :, 7:15], ALU.add, ["pA"], ["pB"])
                        CP("pool", B[0:64, :, 15:23], A[0:64, :, 15:23], ["pA"], ["pB"])
                    STT("dve", v4(Pd[:, :]), B[:, :, 15:23], INVW(kc), uk[:, :, 15:23], ALU.mult, ALU.subtract, ["pB", "u", "vec"], ["rg0"])
                    b = bank()
                    MM(PS[b][:, 0:NS], BD[:, kc, :], Pd[:, :], True, True, ["rg0", "BD"], [("ps", b)])
                    TSC("dve", mixT[:, 3 + kc, 0:NS], PS[b][:, 0:NS], VC(45 * l + 16 + kc), None, ALU.mult, None, ["vec"], [("ps", b), "hnT"])
                for b4 in range(4):
                    for k in range(2):
                        P.dma("sp", s_pool[l, b4, :, k * 128:(k + 1) * 128].rearrange("r p -> p r"), u_s[:, k, b4, 8:23], reads=["u"], tag="spo")
                    for k in range(3):
                        P.dma("sp", s_conv[l, b4, :, k * 128:(k + 1) * 128].rearrange("r p -> p r"), xr_s[:, k, b4, 8:11], reads=["xr"], tag="sco")
                for kc in range(3):
                    tmp = tuple(v4(TS[i][:, :]) for i in range(4))
                    xk = xr_s[:, kc, :, :]

                    def scan_fn(a_, b_, h_, l=l, kc=kc):
                        for b4 in range(4):
                            SCAN(h_[:, b4, :], a_[:, b4, :], b_[:, b4, :], hs0[:, kc, b4:b4 + 1], ["rg1", "rg2", "hs0", "rg3"], ["rg3"])
                        CP("dve", hs0[:, kc, :], h_[:, :, 7], ["rg3"], ["hs0"])
                        TT("dve", v4(mixT[:, 5 + kc, 0:NS]), h_, v4(g_s[:, kc, 0:NS]), ALU.mult, ["rg3", "g"], ["hnT"])

                    mmv = [((lambda ap: ap.rearrange("p a t -> p (a t)")), NS)]
                    rglru_core(l, kc, xk[:, :, 3:11], xk[:, :, 2:10], xk[:, :, 1:9], xk[:, :, 0:8], tmp, mmv, scan_fn)
                for k in range(3):
                    P.dma("sp", s_h[l, :, k * 128:(k + 1) * 128].rearrange("b p -> p b"), hs0[:, k, :], reads=["hs0"], tag="sho")
                dense_tail(l, G)
            final_out(G, lambda r0, nt: y_s[r0:r0 + nt, :])

        convert_weights()
        for j in range(nch):
            plan_layer(0); plan_layer(1)
        if do_sample:
            plan_layer(0); plan_layer(1)
        for j in range(nch):
            prompt_chunk(j)
        for l in range(2):
            for k in range(2):
                P.dma("sp", p_pool[l, :, k * 128:(k + 1) * 128].rearrange("r p -> p r"), utail[:, l, k, :], reads=["utail"], tag="ppo")
            for k in range(3):
                P.dma("sp", p_conv[l, :, k * 128:(k + 1) * 128].rearrange("r p -> p r"), xrtail[:, l, k, :], reads=["xrtail"], tag="pco")
                P.dma("sp", p_h[l:l + 1, k * 128:(k + 1) * 128].rearrange("o p -> p o"), hst[:, l, k:k + 1], reads=["hst"], tag="pho")
        if do_sample:
            P.barrier()
            sample_pass()
        P.final_wait("sp")
        P.emit()
    return nc


def _prep_common(inp):
    f = lambda a: np.ascontiguousarray(np.asarray(a, dtype=np.float32))
    w_in = f(inp["w_in"])
    sw = np.arange(384).reshape(6, 2, 32)[:, ::-1, :].reshape(-1)
    wins, wvs, wouts, wgus, wdns, bds = [], [], [], [], [], []
    vec = np.zeros((128, 128), np.float32)
    for l in range(2):
        W = w_in[l]
        q, k, v = W[:, 0:384], W[:, 384:768], W[:, 768:1152]
        u, xr, gt = W[:, 1152:1408], W[:, 1408:1792], W[:, 1792:2176]
        ext = np.concatenate([q, k, q[:, sw], k[:, sw], u, xr, gt], 1)
        wins.append(_chunkmajor(ext, 8))
        wvs.append(np.ascontiguousarray(v.reshape(8, 128, 384).transpose(1, 0, 2).reshape(128, 8 * 384)))
        wouts.append(_chunkmajor(f(inp["w_out"])[l], 8))
        wgus.append(_chunkmajor(f(inp["w_gu"])[l], 8))
        wdns.append(_chunkmajor(f(inp["w_down"])[l], 22))
        pw, ga, gx = f(inp["pool_w"])[l], f(inp["gate_a_w"])[l], f(inp["gate_x_w"])[l]
        bds.append(np.stack([_blockdiag(pw[0:2]), _blockdiag(pw[2:4])] + [_blockdiag(ga[2 * i:2 * i + 2]) for i in range(3)]
                            + [_blockdiag(gx[2 * i:2 * i + 2]) for i in range(3)], 0))
        o = 45 * l
        vec[:, o:o + 8] = _vec128(f(inp["norm1_g"])[l]); vec[:, o + 8:o + 16] = _vec128(f(inp["norm2_g"])[l])
        vec[:, o + 16:o + 18] = _vec128(f(inp["pool_scale"])[l])
        cw = f(inp["conv_w"])[l]
        for j in range(4):
            vec[:, o + 18 + 3 * j:o + 21 + 3 * j] = _vec128(cw[j])
        vec[:, o + 30:o + 33] = _vec128(f(inp["conv_b"])[l]); vec[:, o + 33:o + 36] = _vec128(f(inp["gate_a_b"])[l])
        vec[:, o + 36:o + 39] = _vec128(f(inp["gate_x_b"])[l]); vec[:, o + 39:o + 42] = _vec128(f(inp["lru_lambda"])[l])
    vec[:, 90:98] = _vec128(f(inp["final_g"]))
    vec[0:64, 98] = 1 / 2; vec[64:128, 98] = 1 / 4; vec[0:64, 99] = 1 / 8; vec[64:128, 99] = 1 / 16
    cosT = np.zeros((NCH, 128, C), np.float32); sinT = np.zeros((NCH, 128, C), np.float32)
    for j in range(NCH):
        cosT[j], sinT[j] = _rope_tables(np.arange(j * C, (j + 1) * C))
    cs, ss = _rope_tables(16384 + np.arange(8))
    cosS = np.tile(cs, (1, 4)); sinS = np.tile(ss, (1, 4))
    p = np.arange(128)
    maskP = np.zeros((128, 17, 128), np.float32)
    for o in range(17):
        maskP[:, o, :] = _mult(128 * o + p[None, :] - p[:, None])
    maskS = np.zeros((128, 17, 8), np.float32)
    for t in range(16):
        maskS[:, t, :] = _mult(2048 + np.arange(8)[None, :] - (128 * t + p[:, None]))
    maskS[0:8, 16, :] = _mult(np.arange(8)[None, :] - np.arange(8)[:, None])
    invcnt = np.zeros((128, 2, 16), np.float32)
    wpp = np.array([[2] * 64 + [4] * 64, [8] * 64 + [16] * 64], np.float32)
    for kc in range(2):
        invcnt[:, kc, :] = 1.0 / np.minimum(np.arange(16)[None, :] + 1, wpp[kc][:, None])
    return dict(w_in=np.stack(wins), w_v=np.stack(wvs), w_out=np.stack(wouts), w_gu=np.stack(wgus), w_dn=np.stack(wdns),
                vecs=vec, bd=np.stack(bds), cosT=cosT, sinT=sinT, cosS=np.ascontiguousarray(cosS), sinS=np.ascontiguousarray(sinS),
                maskP=np.ascontiguousarray(maskP.reshape(128, -1)), maskS=np.ascontiguousarray(maskS.reshape(128, -1)),
                ident=np.eye(128, dtype=np.float32), invcnt=np.ascontiguousarray(invcnt.reshape(128, 32)))


_NC_CACHE = {}


def kernel(**inp):
    f = lambda a: np.ascontiguousarray(np.asarray(a, dtype=np.float32))
    common = _prep_common(inp)
    common["x_p"] = f(inp["x_prompt"])[0]
    xs = f(inp["x_sample"]); ckk = f(inp["cache_win_k"]).reshape(2, 32, 2048, 384); cvv = f(inp["cache_win_v"]).reshape(2, 32, 2048, 384)
    sp_, sc_, sh_ = f(inp["state_pool"]), f(inp["state_conv"]), f(inp["state_rglru"])
    in_maps = []
    for c in range(NCORE):
        m = dict(common)
        sl = slice(4 * c, 4 * c + 4)
        m["x_s"] = np.ascontiguousarray(xs[sl].reshape(NS, D))
        m["ck"] = np.ascontiguousarray(ckk[:, sl]); m["cv"] = np.ascontiguousarray(cvv[:, sl])
        m["st_pool"] = np.ascontiguousarray(sp_[:, sl]); m["st_conv"] = np.ascontiguousarray(sc_[:, sl]); m["st_h"] = np.ascontiguousarray(sh_[:, sl])
        in_maps.append(m)
    if "nc" not in _NC_CACHE:
        _NC_CACHE["nc"] = build()
    res = run_bass_kernel_spmd(_NC_CACHE["nc"], in_maps, core_ids=list(range(NCORE)))
    R = res.results
    cat = lambda name, ax: np.concatenate([R[c][name] for c in range(NCORE)], axis=ax)
    y_prompt = R[0]["y_p"].reshape(1, SEQ, D)
    y_sample = cat("y_s", 0).reshape(32, 8, D)
    p_win_k = R[0]["p_k"].reshape(2, 1, 2048, 6, 64); p_win_v = R[0]["p_v"].reshape(2, 1, 2048, 6, 64)
    p_pool = R[0]["p_pool"].reshape(2, 1, 15, 256); p_conv = R[0]["p_conv"].reshape(2, 1, 3, 384); p_rglru = R[0]["p_h"].reshape(2, 1, 384)
    s_win_k = cat("s_k", 1).reshape(2, 32, 2048, 6, 64); s_win_v = cat("s_v", 1).reshape(2, 32, 2048, 6, 64)
    s_pool = cat("s_pool", 1); s_conv = cat("s_conv", 1); s_rglru = cat("s_h", 1)
    return tuple(np.ascontiguousarray(a.astype(np.float32)) for a in
                 (y_prompt, y_sample, p_win_k, p_win_v, p_pool, p_conv, p_rglru, s_win_k, s_win_v, s_pool, s_conv, s_rglru))
```

```python
import contextlib
import numpy as np
import concourse.bass as bass
import concourse.mybir as mybir
from concourse.bass_utils import run_bass_kernel_spmd

F32 = mybir.dt.float32
BF16 = mybir.dt.bfloat16
AF = mybir.ActivationFunctionType
ALU = mybir.AluOpType

SEM_CAP = 30000
NCORE = 8
D = 1024
SEQ = 16384
C = 1024
NCH = SEQ // C
DFF = 2816
NS = 32
EPS = 1e-6


class Op:
    __slots__ = ("eng", "fn", "deps", "inc", "is_dma", "tag", "sem", "val")

    def __init__(self, eng, fn, is_dma=False, tag=None):
        self.eng, self.fn, self.deps, self.inc = eng, fn, [], False
        self.is_dma, self.tag, self.sem, self.val = is_dma, tag, None, None


class Prog:
    ENG = ("pe", "act", "dve", "pool", "sp")

    def __init__(self, nc):
        self.nc = nc
        self.ops = {e: [] for e in self.ENG}
        self.last_w, self.readers, self.all_ops = {}, {}, []

    def _track(self, op, reads, writes):
        deps = []
        for k in reads:
            w = self.last_w.get(k)
            if w is not None:
                deps.append((w, True))
        for k in writes:
            w = self.last_w.get(k)
            if w is not None:
                deps.append((w, False))
            for r in self.readers.get(k, ()):
                deps.append((r, False))
        for k in writes:
            self.last_w[k] = op
            self.readers[k] = []
        for k in reads:
            self.readers.setdefault(k, []).append(op)
        seen = set()
        for d, raw in deps:
            if d is op or id(d) in seen:
                continue
            if (not d.is_dma) and (not op.is_dma) and d.eng == op.eng:
                if op.eng == "pe" or not raw:
                    continue
            seen.add(id(d))
            op.deps.append(d)
            d.inc = True

    defer = None

    def flush(self, lst, n):
        for _ in range(min(n, len(lst))):
            kind, a = lst.pop(0)
            (self.op if kind == "op" else self.dma)(*a[0], **a[1])

    def op(self, eng, fn, reads=(), writes=()):
        if self.defer is not None:
            self.defer.append(("op", ((eng, fn, reads, writes), {})))
            return None
        o = Op(eng, fn)
        self._track(o, reads, writes)
        self.ops[eng].append(o)
        self.all_ops.append(o)
        return o

    def dma(self, q, out, in_, reads=(), writes=(), tag=None, **kw):
        if self.defer is not None:
            self.defer.append(("dma", ((q, out, in_), dict(reads=reads, writes=writes, tag=tag, **kw))))
            return None
        o = Op(q, lambda e: e.dma_start(out=out, in_=in_, **kw), is_dma=True, tag=tag)
        self._track(o, reads, writes)
        o.inc = True
        self.ops[q].append(o)
        self.all_ops.append(o)
        return o

    def barrier(self):
        lasts = []
        for e in self.ENG:
            for x in reversed(self.ops[e]):
                if not x.is_dma and x.fn is not None:
                    lasts.append(x)
                    break
        lastd = {}
        for x in self.all_ops:
            if x.is_dma:
                lastd[x.tag] = x
        deps = lasts + list(lastd.values())
        for d in deps:
            d.inc = True
        for e in self.ENG:
            o = Op(e, None)
            o.deps = [d for d in deps if not (d.eng == e and not d.is_dma and e == "pe")]
            self.ops[e].append(o)
            self.all_ops.append(o)

    def final_wait(self, eng="sp"):
        o = Op(eng, None)
        last = {}
        for x in self.all_ops:
            if x.is_dma:
                last[x.tag] = x
        o.deps = list(last.values())
        self.ops[eng].append(o)
        self.all_ops.append(o)

    def emit(self):
        nc = self.nc
        with contextlib.ExitStack() as st:
            ctr = [0]

            def new_sem(name):
                ctr[0] += 1
                return st.enter_context(nc.semaphore(f"{name}_{ctr[0]}"))

            cur = {e: [new_sem("e_" + e), 0] for e in self.ENG}
            tagsem = {}
            for o in self.all_ops:
                if not o.inc:
                    continue
                if o.is_dma:
                    if o.tag not in tagsem or tagsem[o.tag][1] >= SEM_CAP:
                        tagsem[o.tag] = [new_sem("d"), 0]
                    ts = tagsem[o.tag]
                    ts[1] += 16
                    o.sem, o.val = ts[0], ts[1]
                else:
                    c = cur[o.eng]
                    if c[1] >= SEM_CAP:
                        c = cur[o.eng] = [new_sem("e_" + o.eng), 0]
                    c[1] += 1
                    o.sem, o.val = c[0], c[1]

            def run(engname):
                def body(eng):
                    waited = {}
                    for o in self.ops[engname]:
                        for d in o.deps:
                            key = id(d.sem)
                            if waited.get(key, 0) >= d.val:
                                continue
                            waited[key] = d.val
                            eng.wait_ge(d.sem, d.val)
                        if o.fn is None:
                            continue
                        ins = o.fn(eng)
                        if o.inc:
                            ins.then_inc(o.sem, 16 if o.is_dma else 1)
                return body

            with nc.Block() as block:
                block.tensor(run("pe"))
                block.scalar(run("act"))
                block.vector(run("dve"))
                block.gpsimd(run("pool"))
                block.sync(run("sp"))


def _mult(delta):
    delta = np.asarray(delta)
    m = ((delta >= 0) & (delta <= 128)).astype(np.float32)
    m += ((delta >= 0) & (delta <= 512) & (delta % 4 == 0))
    m += ((delta >= 0) & (delta <= 2048) & (delta % 16 == 0))
    return m.astype(np.float32)


def _rope_tables(pos):
    half = 32
    inv = np.power(np.float32(10000.0), (-2.0 * np.arange(half, dtype=np.float32) / np.float32(64))).astype(np.float32)
    ang = pos.astype(np.float32)[None, :] * inv[:, None]
    cos = np.cos(ang).astype(np.float32)
    sin = np.sin(ang).astype(np.float32)
    cos128 = np.concatenate([cos, cos, cos, cos], 0)
    sin128 = np.concatenate([-sin, sin, -sin, sin], 0)
    return np.ascontiguousarray(cos128), np.ascontiguousarray(sin128)


def _chunkmajor(w, nk):
    K, M = w.shape
    return np.ascontiguousarray(w.reshape(nk, 128, M // 128, 128).transpose(2, 1, 0, 3).reshape(M // 128, 128, nk * 128))


def _vec128(v):
    return np.ascontiguousarray(v.reshape(-1, 128).T)


def _blockdiag(w):
    o = np.zeros((128, 128), np.float32)
    o[:64, :64] = w[0]
    o[64:, 64:] = w[1]
    return o


def build(nch=NCH, do_sample=True):
    nc = bass.Bass("TRN2", target_bir_lowering=False)
    di = lambda n, s, dt=F32: nc.dram_tensor(n, list(s), dt, kind="ExternalInput").ap()
    do = lambda n, s: nc.dram_tensor(n, list(s), F32, kind="ExternalOutput").ap()
    x_p = di("x_p", [SEQ, D]); x_s = di("x_s", [NS, D])
    ck = di("ck", [2, 4, 2048, 384]); cv = di("cv", [2, 4, 2048, 384])
    st_pool = di("st_pool", [2, 4, 15, 256]); st_conv = di("st_conv", [2, 4, 3, 384]); st_h = di("st_h", [2, 4, 384])
    w_in = di("w_in", [2, 20, 128, 1024]); w_v = di("w_v", [2, 128, 8 * 384])
    w_out = di("w_out", [2, 8, 128, 1024]); w_gu = di("w_gu", [2, 44, 128, 1024]); w_dn = di("w_dn", [2, 8, 128, DFF])
    vecs = di("vecs", [128, 128]); bd = di("bd", [2, 8, 128, 128])
    cosT = di("cosT", [NCH, 128, C]); sinT = di("sinT", [NCH, 128, C])
    cosS = di("cosS", [128, NS]); sinS = di("sinS", [128, NS])
    maskP = di("maskP", [128, 17 * 128]); maskS = di("maskS", [128, 17 * 8])
    ident_d = di("ident", [128, 128]); invcnt_d = di("invcnt", [128, 2 * 16])

    y_p = do("y_p", [SEQ, D]); y_s = do("y_s", [NS, D])
    p_k = do("p_k", [2, 2048, 384]); p_v = do("p_v", [2, 2048, 384])
    p_pool = do("p_pool", [2, 15, 256]); p_conv = do("p_conv", [2, 3, 384]); p_h = do("p_h", [2, 384])
    s_k = do("s_k", [2, 4, 2048, 384]); s_v = do("s_v", [2, 4, 2048, 384])
    s_pool = do("s_pool", [2, 4, 15, 256]); s_conv = do("s_conv", [2, 4, 3, 384]); s_h = do("s_h", [2, 4, 384])
    wb_in = nc.dram_tensor("wb_in", [2, 20, 128, 1024], BF16); wb_v = nc.dram_tensor("wb_v", [2, 128, 8 * 384], BF16)
    wb_out = nc.dram_tensor("wb_out", [2, 8, 128, 1024], BF16); wb_gu = nc.dram_tensor("wb_gu", [2, 44, 128, 1024], BF16)
    wb_dn = nc.dram_tensor("wb_dn", [2, 8, 128, DFF], BF16)
    kt_ring = nc.dram_tensor("kt_ring", [2, 3, 128, 3 * C], BF16)
    v1_ring = nc.dram_tensor("v1_ring", [2, 3, 128, 8 * 390], BF16)

    with contextlib.ExitStack() as st:
        st.enter_context(nc.allow_non_contiguous_dma(reason="small state/vec transposes"))
        sb = lambda name, shape, dt=F32: st.enter_context(nc.sbuf_tensor(name, list(shape), dt))
        xT = sb("xT", [128, 8, C]); hnT = sb("hnT", [128, 8, C], BF16)
        mixT = hnT
        ARENA = sb("ARENA", [128, 12800])
        AB = ARENA[:, :].bitcast(BF16)
        kT = AB[:, 0:9216].rearrange("p (k n) -> p k n", k=3)
        V1 = AB[:, 9216:18576].rearrange("p (t h d) -> p t h d", t=24, h=6)
        qT = AB[:, 18576:21648].rearrange("p (k n) -> p k n", k=3)
        g_sb = AB[:, 21648:24720].rearrange("p (k n) -> p k n", k=3)
        hT = AB[:, 0:22528].rearrange("p (k n) -> p k n", k=22)
        yf = ARENA[:, 0:4096].rearrange("p (k n) -> p k n", k=8)
        stage = ARENA[:, 0:6144].rearrange("p (t f) -> p t f", t=16)
        kTc = AB[:, 12288:18432].rearrange("p (k n) -> p k n", k=3)
        V1c = AB[:, 18432:24672].rearrange("p (t h d) -> p t h d", t=16, h=6)
        qTs = AB[:, 24672:24768].rearrange("p (k n) -> p k n", k=3)
        g_s = AB[:, 24768:24864].rearrange("p (k n) -> p k n", k=3)
        u_sb = sb("u_sb", [128, 2, 15 + C]); xr_sb = sb("xr_sb", [128, 3, 3 + C])
        T = [sb(f"T{i}", [128, 15 + C]) for i in range(4)]
        NSLOT = 5
        WR = [sb(f"WR{i}", [128, 1024], BF16) for i in range(NSLOT)]
        Wv = sb("Wv", [128, 8, 384], BF16)
        cos_sb = sb("cos_sb", [128, C]); sin_sb = sb("sin_sb", [128, C])
        sq = sb("sq", [128, 2, 512], BF16); rs = sb("rs", [128, 512])
        tA = sb("tA", [128, 512]); tB = sb("tB", [128, 512]); tA2 = sb("tA2", [128, 512]); tB2 = sb("tB2", [128, 512])
        es = [sb(f"es{i}", [128, 512], BF16) for i in range(5)]
        mP = sb("mP", [128, 17 * 128], BF16); mS = sb("mS", [128, 17 * 8], BF16)
        ident = sb("ident_sb", [128, 128]); ones_b = sb("ones_b", [128, 128], BF16)
        vec = sb("vec", [128, 128]); BD = sb("BD", [128, 8, 128]); invcnt = sb("invcnt_sb", [128, 2, 16])
        osb = sb("osb", [128, 6, 64]); rden = sb("rden", [128, 6]); den = sb("den", [128, 6])
        ytok = sb("ytok", [128, D]); vst = sb("vst", [128, 384])
        xtok = ytok
        utail = sb("utail", [128, 2, 2, 15]); xrtail = sb("xrtail", [128, 2, 3, 3]); hst = sb("hst", [128, 2, 3])
        V1n = T[0][0:8, 0:780].bitcast(BF16).rearrange("p (t h d) -> p t h d", t=4, h=6)
        u_s = T[0][:, 780:964].rearrange("p (k a t) -> p k a t", k=2, a=4)
        xr_s = T[1][:, 0:132].rearrange("p (k a t) -> p k a t", k=3, a=4)
        TSP = [T[1][:, 132 + 92 * i:224 + 92 * i].rearrange("p (a t) -> p a t", a=4) for i in range(2)]
        TS = [T[1][:, 316 + 32 * i:348 + 32 * i] for i in range(4)]
        kTs = T[1][:, 444:492].bitcast(BF16).rearrange("p (k n) -> p k n", k=3)
        kfs = T[1][:, 492:588].rearrange("p (k n) -> p k n", k=3)
        hs0 = T[1][:, 588:600].rearrange("p (k a) -> p k a", k=3)
        PS = [st.enter_context(nc.psum_tensor(f"ps{i}", [128, 512], F32)) for i in range(8)]

        P = Prog(nc)
        state = {"bank": 0, "slot": 0, "ev": 0, "es": 0}

        def bank():
            b = state["bank"]; state["bank"] = (b + 1) % 6
            return b

        def acc_bank():
            state["acc"] = 13 - state.get("acc", 7)
            return state["acc"]

        def ev_eng():
            state["ev"] ^= 1
            return "act" if state["ev"] else "dve"

        def MM(out, lhsT, rhs, start, stop, reads, writes):
            P.op("pe", lambda e: e.matmul(out, lhsT, rhs, start=start, stop=stop), reads, writes)

        def TR(out, in_, idn, reads, writes):
            P.op("pe", lambda e: e.transpose(out, in_, idn), reads, writes)

        def ACT(out, in_, func, reads, writes, bias=None, scale=None):
            kw = {}
            if bias is not None: kw["bias"] = bias
            if scale is not None: kw["scale"] = scale
            P.op("act", lambda e: e.activation(out, in_, func, **kw), reads, writes)

        def TT(eng, out, a, b, op, reads, writes):
            P.op(eng, lambda e: e.tensor_tensor(out, a, b, op), reads, writes)

        def TSC(eng, out, a, s1, s2, op0, op1, reads, writes):
            if op1 is None:
                P.op(eng, lambda e: e.tensor_scalar(out, a, s1, None, op0), reads, writes)
            else:
                P.op(eng, lambda e: e.tensor_scalar(out, a, s1, s2, op0, op1), reads, writes)

        def STT(eng, out, a, s, b, op0, op1, reads, writes):
            P.op(eng, lambda e: e.scalar_tensor_tensor(out, a, s, b, op0, op1), reads, writes)

        def CP(eng, out, in_, reads, writes):
            if eng == "act":
                P.op("act", lambda e: e.activation(out, in_, AF.Copy), reads, writes)
            else:
                P.op(eng, lambda e: e.tensor_copy(out, in_), reads, writes)

        def RECIP(out, in_, reads, writes):
            P.op("dve", lambda e: e.reciprocal(out, in_), reads, writes)

        def SCAN(out, a, b, init, reads, writes):
            P.op("dve", lambda e: e.tensor_tensor_scan(out, a, b, init, ALU.mult, ALU.add), reads, writes)

        def MEMSET(ap, v, writes):
            P.op("pool", lambda e: e.memset(ap, v), (), writes)

        VC = lambda c: vec[:, c:c + 1]
        INVW = lambda k: VC(98 + k)
        FG = lambda k: VC(90 + k)

        P.dma("sp", vec[:, :], vecs, writes=["vec"], tag="c0")
        P.dma("sp", ident[:, :], ident_d, writes=["ident"], tag="c1")
        P.dma("sp", invcnt[:, :, :].rearrange("p a b -> p (a b)"), invcnt_d, writes=["invcnt"], tag="c2")
        P.dma("pool", mP[:, :], maskP, writes=["mP"], tag="c3")
        P.dma("pool", mS[:, :], maskS, writes=["mS"], tag="c4")
        MEMSET(ones_b[:, :], 1.0, ["ones"])
        MEMSET(utail[:, :, :, :].rearrange("p a b c -> p (a b c)"), 0.0, ["utail"])
        MEMSET(xrtail[:, :, :, :].rearrange("p a b c -> p (a b c)"), 0.0, ["xrtail"])
        MEMSET(hst[:, :, :].rearrange("p a b -> p (a b)"), 0.0, ["hst"])
        for l in range(2):
            lam = vec[:, 45 * l + 39:45 * l + 42]
            ACT(lam, lam, AF.Exp, ["vec"], ["vec"], scale=-1.0)
            ACT(lam, lam, AF.Ln, ["vec"], ["vec"], bias=1.0, scale=1.0)
            TSC("dve", vec[:, 45 * l + 42:45 * l + 45], lam, -16.0, None, ALU.mult, None, ["vec"], ["vec"])
            TSC("dve", lam, lam, -8.0, None, ALU.mult, None, ["vec"], ["vec"])
        if do_sample:
            for l in range(2):
                for b4 in range(4):
                    P.dma("sp", s_k[l, b4, 0:2040, :].rearrange("(a r) f -> a (r f)", a=120), ck[l, b4, 8:2048, :].rearrange("(a r) f -> a (r f)", a=120), tag="sk")
                    P.dma("sp", s_v[l, b4, 0:2040, :].rearrange("(a r) f -> a (r f)", a=120), cv[l, b4, 8:2048, :].rearrange("(a r) f -> a (r f)", a=120), tag="sv")

        wplan = []
        AHEAD = 2

        def plan_layer(l):
            for j in range(3):
                for base in (0, 3):
                    wplan.append((wb_in.ap()[l, base + j], 1024)); wplan.append((wb_in.ap()[l, base + 6 + j], 1024))
            for c in (12, 13, 14, 15, 16, 17, 18, 19):
                wplan.append((wb_in.ap()[l, c], 1024))
            for m in range(8):
                wplan.append((wb_out.ap()[l, m], 1024))
            for m in range(22):
                wplan.append((wb_gu.ap()[l, m], 1024)); wplan.append((wb_gu.ap()[l, 22 + m], 1024))
            for m in range(8):
                wplan.append((wb_dn.ap()[l, m, :, 0:1024], 1024)); wplan.append((wb_dn.ap()[l, m, :, 1024:2048], 1024))
                wplan.append((wb_dn.ap()[l, m, :, 2048:DFF], DFF - 2048))

        def load_w(dram_ap, ncols=1024):
            i = state.get("wc", 0); state["wc"] = i + 1
            while state.get("wi", 0) < min(len(wplan), i + 1 + AHEAD):
                k = state.get("wi", 0)
                ap, nc_ = wplan[k]
                P.dma("sp", WR[k % NSLOT][:, 0:nc_], ap, reads=["wb"], writes=[("w", k % NSLOT)], tag=f"w{k % NSLOT}")
                state["wi"] = k + 1
            assert wplan[i][1] == ncols, (i, wplan[i][1], ncols)
            return i % NSLOT

        def convert_weights():
            def conv(dst, src, ncols):
                s = state["slot"]; state["slot"] = (s + 1) % NSLOT
                P.dma("pool", WR[s][:, 0:ncols], src, writes=[("w", s)], tag=f"cv{s}")
                P.dma("sp", dst, WR[s][:, 0:ncols], reads=[("w", s)], writes=["wb"], tag="wbs")
            for l in range(2):
                for m in range(20):
                    conv(wb_in.ap()[l, m], w_in[l, m], 1024)
                for i in range(3):
                    conv(wb_v.ap()[l, :, i * 1024:(i + 1) * 1024], w_v[l, :, i * 1024:(i + 1) * 1024], 1024)
                for m in range(8):
                    conv(wb_out.ap()[l, m], w_out[l, m], 1024)
                for m in range(44):
                    conv(wb_gu.ap()[l, m], w_gu[l, m], 1024)
                for m in range(8):
                    for (a0, a1) in ((0, 1024), (1024, 2048), (2048, DFF)):
                        conv(wb_dn.ap()[l, m, :, a0:a1], w_dn[l, m, :, a0:a1], a1 - a0)

        def mm(slots, nk, src, skey, c0, n):
            b = bank()
            kc = 0
            for (s, cnt) in slots:
                for i in range(cnt):
                    MM(PS[b][:, 0:n], WR[s][:, i * 128:(i + 1) * 128], src[:, kc, c0:c0 + n], kc == 0, kc == nk - 1,
                       [("w", s)] + (list(skey) if isinstance(skey, (list, tuple)) else [skey]), [("ps", b)])
                    kc += 1
            return b

        def load_x(dram_rows, ntok, c0):
            P.dma("sp", xtok[0:ntok, :], dram_rows, writes=["ytok"], tag="xtok")
            for half in range(2):
                b = bank()
                for i in range(4):
                    k = half * 4 + i
                    TR(PS[b][:, i * 128:i * 128 + ntok], xtok[0:ntok, k * 128:(k + 1) * 128], ident[0:ntok, 0:ntok], ["ytok", "ident"], [("ps", b)])
                CP(ev_eng(), xT[:, half * 4:half * 4 + 4, c0:c0 + ntok], PS[b][:, :].rearrange("p (i n) -> p i n", i=4)[:, :, 0:ntok], [], [("ps", b), "xT"])

        def rmsnorm(gfn, groups, dst, dkey):
            for (c0, n) in groups:
                b = bank()
                for kc in range(8):
                    ACT(sq[:, kc % 2, 0:n], xT[:, kc, c0:c0 + n], AF.Square, ["xT"], [("sq", kc % 2)])
                    MM(PS[b][:, 0:n], ones_b[:, :], sq[:, kc % 2, 0:n], kc == 0, kc == 7, [("sq", kc % 2), "ones"], [("ps", b)])
                ACT(rs[:, 0:n], PS[b][:, 0:n], AF.Sqrt, [], [("ps", b), "rs"], bias=EPS, scale=1.0 / D)
                RECIP(rs[:, 0:n], rs[:, 0:n], ["rs"], ["rs"])
                for kc in range(8):
                    STT("dve", dst(kc, c0, n), xT[:, kc, c0:c0 + n], gfn(kc), rs[:, 0:n], ALU.mult, ALU.mult, ["xT", "rs", "vec"], [dkey])

        def dense_tail(l, groups):
            for m in range(8):
                s = load_w(wb_out.ap()[l, m])
                for (c0, n) in groups:
                    b = mm([(s, 8)], 8, mixT, ["hnT", "mixA", "mixP", "mixR"], c0, n)
                    TT("dve", xT[:, m, c0:c0 + n], xT[:, m, c0:c0 + n], PS[b][:, 0:n], ALU.add, [], [("ps", b), "xT"])
            rmsnorm(lambda kc: VC(45 * l + 8 + kc), groups, lambda kc, c0, n: hnT[:, kc, c0:c0 + n], "hnT")
            P.barrier()
            for m in range(22):
                sg = load_w(wb_gu.ap()[l, m]); su = load_w(wb_gu.ap()[l, 22 + m])
                for (c0, n) in groups:
                    bg = mm([(sg, 8)], 8, hnT, "hnT", c0, n)
                    bu = mm([(su, 8)], 8, hnT, "hnT", c0, n)
                    ACT(tA[:, 0:n], PS[bg][:, 0:n], AF.Silu, [], [("ps", bg), "tA"])
                    TT("dve", hT[:, m, c0:c0 + n], tA[:, 0:n], PS[bu][:, 0:n], ALU.mult, ["tA"], [("ps", bu), ("hT", m)])
            for m in range(8):
                ss = [(load_w(wb_dn.ap()[l, m, :, 0:1024]), 8), (load_w(wb_dn.ap()[l, m, :, 1024:2048]), 8), (load_w(wb_dn.ap()[l, m, :, 2048:DFF], DFF - 2048), 6)]
                for (c0, n) in groups:
                    b = bank()
                    kc = 0
                    for (s, cnt) in ss:
                        for i in range(cnt):
                            MM(PS[b][:, 0:n], WR[s][:, i * 128:(i + 1) * 128], hT[:, kc, c0:c0 + n], kc == 0, kc == 21, [("w", s), ("hT", kc)], [("ps", b)])
                            kc += 1
                    TT("dve", xT[:, m, c0:c0 + n], xT[:, m, c0:c0 + n], PS[b][:, 0:n], ALU.add, [], [("ps", b), "xT"])
            P.barrier()

        def final_out(groups, out_rows_fn):
            for (c0, n) in groups:
                rmsnorm(FG, [(c0, n)], lambda kc, c0_, n_: yf[:, kc, 0:n_], "yf")
                for t0 in range(0, n, 128):
                    nt = min(128, n - t0)
                    for half in range(2):
                        b = bank()
                        for i in range(4):
                            k = half * 4 + i
                            TR(PS[b][0:nt, i * 128:(i + 1) * 128], yf[:, k, t0:t0 + nt], ident[:, :], ["yf", "ident"], [("ps", b)])
                        CP(ev_eng(), ytok[0:nt, half * 512:(half + 1) * 512], PS[b][0:nt, :], [], [("ps", b), "ytok"])
                    P.dma("sp", out_rows_fn(c0 + t0, nt), ytok[0:nt, :], reads=["ytok"], tag="yout")
            P.barrier()

        def w_in_feature(l, groups, cos_d, sin_d, qdst, kdst, kcol0, udst, xrdst, pview, kf32_hook, gdst):
            for (c0, n) in groups:
                pass
            for j in range(3):
                for (dst, base, col0, isk) in ((qdst, 0, 0, False), (kdst, 3, kcol0, True)):
                    s1 = load_w(wb_in.ap()[l, base + j]); s2 = load_w(wb_in.ap()[l, base + 6 + j])
                    for (c0, n) in groups:
                        b1 = mm([(s1, 8)], 8, hnT, "hnT", c0, n)
                        b2 = mm([(s2, 8)], 8, hnT, "hnT", c0, n)
                        state["rp"] = state.get("rp", 0) ^ 1
                        ta, tb, ka, kb = (tA, tB, "tA", "tB") if state["rp"] else (tA2, tB2, "tA2", "tB2")
                        TT("dve", ta[:, 0:n], PS[b1][:, 0:n], cos_sb[:, c0:c0 + n], ALU.mult, ["cos"], [("ps", b1), ka])
                        TT("dve", tb[:, 0:n], PS[b2][:, 0:n], sin_sb[:, c0:c0 + n], ALU.mult, ["sin"], [("ps", b2), kb])
                        TT("pool", ta[:, 0:n], ta[:, 0:n], tb[:, 0:n], ALU.add, [kb], [ka])
                        CP("act", dst[:, j, col0 + c0:col0 + c0 + n], ta[:, 0:n], [ka], ["kT" if isk else "qT"])
                        if isk and kf32_hook is not None:
                            kf32_hook(j, c0, n, ta, tb, ka, kb)
            for kc in range(2):
                s = load_w(wb_in.ap()[l, 12 + kc])
                for (c0, n) in groups:
                    b = mm([(s, 8)], 8, hnT, "hnT", c0, n)
                    CP("act", udst(kc, c0, n), pview(PS[b][:, 0:n]), [], [("ps", b), "u"])
            for kc in range(3):
                s = load_w(wb_in.ap()[l, 14 + kc])
                for (c0, n) in groups:
                    b = mm([(s, 8)], 8, hnT, "hnT", c0, n)
                    CP("act", xrdst(kc, c0, n), pview(PS[b][:, 0:n]), [], [("ps", b), "xr"])
            for kc in range(3):
                s = load_w(wb_in.ap()[l, 17 + kc])
                for (c0, n) in groups:
                    b = mm([(s, 8)], 8, hnT, "hnT", c0, n)
                    ACT(gdst[:, kc, c0:c0 + n], PS[b][:, 0:n], AF.Gelu, [], [("ps", b), "g"])

        def load_layer_consts(l):
            P.dma("sp", Wv[:, :, :].rearrange("p k f -> p (k f)"), wb_v.ap()[l], reads=["wb"], writes=["Wv"], tag="wv")
            P.dma("sp", BD[:, :, :], bd[l].rearrange("a p f -> p a f"), writes=["BD"], tag="bd")

        def rglru_core(l, kc, xr0, xr1, xr2, xr3, tmp, mmviews, scan_fn):
            xc, a_, b_, w_ = tmp
            cw = lambda j: VC(45 * l + 18 + 3 * j + kc)
            TSC("pool", xc, xr0, cw(3), VC(45 * l + 30 + kc), ALU.mult, ALU.add, ["xr", "vec"], ["rg0"])
            for j, sh in ((2, xr1), (1, xr2), (0, xr3)):
                STT("dve", xc, sh, cw(j), xc, ALU.mult, ALU.add, ["xr", "vec", "rg0"], ["rg0"])
            for (view, n) in mmviews:
                ba = bank()
                MM(PS[ba][:, 0:n], BD[:, 2 + kc, :], view(xc), True, True, ["rg0", "BD"], [("ps", ba)])
                bx = bank()
                MM(PS[bx][:, 0:n], BD[:, 5 + kc, :], view(xc), True, True, ["rg0", "BD"], [("ps", bx)])
                ACT(view(w_), PS[ba][:, 0:n], AF.Sigmoid, ["vec"], [("ps", ba), "rg3"], bias=VC(45 * l + 33 + kc))
                ACT(view(b_), PS[bx][:, 0:n], AF.Sigmoid, ["vec"], [("ps", bx), "rg2"], bias=VC(45 * l + 36 + kc))
            ACT(a_, w_, AF.Exp, ["rg3", "vec"], ["rg1"], scale=VC(45 * l + 39 + kc))
            ACT(w_, w_, AF.Exp, ["rg3", "vec"], ["rg3"], scale=VC(45 * l + 42 + kc))
            TSC("dve", w_, w_, -1.0, 1.0, ALU.mult, ALU.add, ["rg3"], ["rg3"])
            P.op("dve", lambda e: e.tensor_scalar_max(w_, w_, 0.0), ["rg3"], ["rg3"])
            ACT(w_, w_, AF.Sqrt, ["rg3"], ["rg3"])
            TT("pool", b_, b_, w_, ALU.mult, ["rg2", "rg3"], ["rg2"])
            TT("pool", b_, b_, xc, ALU.mult, ["rg2", "rg0"], ["rg2"])
            scan_fn(a_, b_, w_)

        def prompt_chunk(j):
            G = [(0, 512), (512, 512)]
            for t in range(8):
                load_x(x_p[j * C + t * 128:j * C + (t + 1) * 128, :], 128, t * 128)
            P.dma("sp", cos_sb[:, :], cosT[j], writes=["cos"], tag="cos")
            P.dma("sp", sin_sb[:, :], sinT[j], writes=["sin"], tag="sin")
            for l in range(2):
                load_layer_consts(l)
                MEMSET(V1[:, :, :, 64:65], 1.0, ["V1"])
                for back in (2, 1):
                    if j - back >= 0:
                        sl = (j - back) % 3
                        P.dma("sp", kT[:, :, (2 - back) * C:(3 - back) * C], kt_ring.ap()[l, sl].rearrange("p (k n) -> p k n", k=3),
                              reads=[("ktr", l, sl)], writes=["kT"], tag="ktl")
                        P.dma("sp", V1[:, (2 - back) * 8:(3 - back) * 8, :, :], v1_ring.ap()[l, sl].rearrange("p (t h d) -> p t h d", t=8, h=6),
                              reads=[("v1r", l, sl)], writes=["V1"], tag="v1l")
                rmsnorm(lambda kc: VC(45 * l + kc), G, lambda kc, c0, n: hnT[:, kc, c0:c0 + n], "hnT")
                want = (j >= nch - 2)

                def kf32_hook(jc, c0, n, ta, tb, ka, kb, l=l):
                    if not want:
                        return
                    b = bank()
                    for i in range(n // 128):
                        TR(PS[b][:, i * 128:(i + 1) * 128], ta[:, i * 128:(i + 1) * 128], ident[:, :], [ka, "ident"], [("ps", b)])
                    CP("dve", tb[:, 0:n], PS[b][:, 0:n], [], [("ps", b), kb])
                    r0 = (j - (nch - 2)) * C + c0
                    P.dma("sp", p_k[l, r0:r0 + n, jc * 128:(jc + 1) * 128].rearrange("(i p) f -> p i f", p=128),
                          tb[:, 0:n].rearrange("p (i f) -> p i f", f=128), reads=[kb], tag="pk")

                w_in_feature(l, G, cosT[j], sinT[j], qT, kT, 2 * C,
                             lambda kc, c0, n: u_sb[:, kc, 15 + c0:15 + c0 + n],
                             lambda kc, c0, n: xr_sb[:, kc, 3 + c0:3 + c0 + n], lambda ap: ap, kf32_hook, g_sb)
                for t in range(8):
                    b = bank()
                    for kc in range(8):
                        MM(PS[b][:, 0:384], hnT[:, kc, t * 128:(t + 1) * 128], Wv[:, kc, :], kc == 0, kc == 7, ["hnT", "Wv"], [("ps", b)])
                    if want:
                        CP("act", vst[:, :], PS[b][:, 0:384], [], [("ps", b), "vst"])
                        r0 = (j - (nch - 2)) * C + t * 128
                        P.dma("sp", p_v[l, r0:r0 + 128, :], vst[:, :], reads=["vst"], tag="pv")
                    CP("dve", V1[:, 16 + t, :, 0:64], PS[b][:, 0:384].rearrange("p (h d) -> p h d", h=6), [], [("ps", b), "V1"])
                sl = j % 3
                P.dma("sp", kt_ring.ap()[l, sl].rearrange("p (k n) -> p k n", k=3), kT[:, :, 2 * C:3 * C], reads=["kT"], writes=[("ktr", l, sl)], tag="kts")
                P.dma("sp", v1_ring.ap()[l, sl].rearrange("p (t h d) -> p t h d", t=8, h=6), V1[:, 16:24, :, :], reads=["V1"], writes=[("v1r", l, sl)], tag="v1s")

                P.barrier()
                P.defer = []
                L = 15 + C
                CP("pool", u_sb[:, :, 0:15], utail[:, l, :, :], ["utail"], ["u"])
                for kc in range(2):
                    A, B, Pd = T[0], T[1], T[2]
                    uk = u_sb[:, kc, :]
                    TT("pool", A[:, 1:L], uk[:, 1:L], uk[:, 0:L - 1], ALU.add, ["u"], ["rg0"])
                    if kc == 0:
                        TT("pool", B[64:128, 3:L], A[64:128, 3:L], A[64:128, 1:L - 2], ALU.add, ["rg0"], ["rg1"])
                        CP("pool", B[0:64, 15:L], A[0:64, 15:L], ["rg0"], ["rg1"])
                    else:
                        TT("pool", B[:, 3:L], A[:, 3:L], A[:, 1:L - 2], ALU.add, ["rg0"], ["rg1"])
                        TT("pool", A[:, 7:L], B[:, 7:L], B[:, 3:L - 4], ALU.add, ["rg1"], ["rg0"])
                        TT("pool", B[64:128, 15:L], A[64:128, 15:L], A[64:128, 7:L - 8], ALU.add, ["rg0"], ["rg1"])
                        CP("pool", B[0:64, 15:L], A[0:64, 15:L], ["rg0"], ["rg1"])
                    STT("dve", Pd[:, 15:L], B[:, 15:L], INVW(kc), uk[:, 15:L], ALU.mult, ALU.subtract, ["rg1", "u", "vec"], ["rg2"])
                    if j == 0:
                        TT("dve", Pd[:, 15:31], B[:, 15:31], invcnt[:, kc, :], ALU.mult, ["rg1", "invcnt"], ["rg2"])
                        TT("dve", Pd[:, 15:31], Pd[:, 15:31], uk[:, 15:31], ALU.subtract, ["u", "rg2"], ["rg2"])
                    for (c0, n) in G:
                        b = bank()
                        MM(PS[b][:, 0:n], BD[:, kc, :], Pd[:, 15 + c0:15 + c0 + n], True, True, ["rg2", "BD"], [("ps", b)])
                        TSC("dve", mixT[:, 3 + kc, c0:c0 + n], PS[b][:, 0:n], VC(45 * l + 16 + kc), None, ALU.mult, None, ["vec"], [("ps", b), "mixP"])
                CP("pool", utail[:, l, :, :], u_sb[:, :, C:C + 15], ["u"], ["utail"])

                CP("pool", xr_sb[:, :, 0:3], xrtail[:, l, :, :], ["xrtail"], ["xr"])
                for kc in range(3):
                    tmp = tuple(T[i][:, 0:C] for i in range(4))
                    xk = xr_sb[:, kc, :]

                    def scan_fn(a_, b_, h_, l=l, kc=kc):
                        SCAN(h_, a_, b_, hst[:, l, kc:kc + 1], ["rg1", "rg2", "hst", "rg3"], ["rg3"])
                        CP("dve", hst[:, l, kc:kc + 1], h_[:, C - 1:C], ["rg3"], ["hst"])
                        TT("dve", mixT[:, 5 + kc, :], h_, g_sb[:, kc, :], ALU.mult, ["rg3", "g"], ["mixR"])

                    mmv = [((lambda ap, c0=c0, n=n: ap[:, c0:c0 + n]), n) for (c0, n) in G]
                    rglru_core(l, kc, xk[:, 3:3 + C], xk[:, 2:2 + C], xk[:, 1:1 + C], xk[:, 0:C], tmp, mmv, scan_fn)
                CP("pool", xrtail[:, l, :, :], xr_sb[:, :, C:C + 3], ["xr"], ["xrtail"])
                side = P.defer; P.defer = None
                jobs = []
                for tq in range(8):
                    Tg = 8 * j + tq
                    offs = [o for o in range(17) if Tg - o >= 0]
                    bacc = acc_bank()
                    batches = [offs[i:i + 4] for i in range(0, len(offs), 4)]
                    for h in range(6):
                        for bi_, ob in enumerate(batches):
                            jobs.append((tq, h, ob, bi_ == 0, bi_ == len(batches) - 1, bacc, h == 5 and bi_ == len(batches) - 1))
                LAG = 3
                P.flush(side, len(side))
                per = -(-len(side) // max(1, len(jobs)))
                inflight = []
                for idx in range(len(jobs) + LAG):
                    if idx < len(jobs):
                        tq, h, ob, first, last, bacc, endq = jobs[idx]
                        jc, hp = h // 2, 64 * (h % 2)
                        b = bank()
                        nb = len(ob)
                        for i, o in enumerate(ob):
                            kt_ = 16 + tq - o
                            MM(PS[b][:, i * 128:(i + 1) * 128], kT[hp:hp + 64, jc, kt_ * 128:(kt_ + 1) * 128],
                               qT[hp:hp + 64, jc, tq * 128:(tq + 1) * 128], True, True, ["kT", "qT"], [("ps", b)])
                        ei = state["es"]; state["es"] = (ei + 1) % 5
                        ACT(es[ei][:, 0:nb * 128], PS[b][:, 0:nb * 128], AF.Exp, [], [("ps", b), ("es", ei)], scale=0.125)
                        o0 = ob[0]
                        state["mk"] = (state.get("mk", 0) + 1) % 3
                        TT("pool" if state["mk"] == 0 else "dve", es[ei][:, 0:nb * 128], es[ei][:, 0:nb * 128], mP[:, o0 * 128:(o0 + nb) * 128], ALU.mult, ["mP"], [("es", ei)])
                        inflight.append((jobs[idx], ei))
                        P.flush(side, per)
                    if idx >= LAG:
                        (tq, h, ob, first, last, bacc, endq), ei = inflight.pop(0)
                        for i, o in enumerate(ob):
                            kt_ = 16 + tq - o
                            MM(PS[bacc][:, h * 65:(h + 1) * 65], es[ei][:, i * 128:(i + 1) * 128], V1[:, kt_, h, :],
                               first and i == 0, last and i == len(ob) - 1, [("es", ei), "V1"], [("ps", bacc)])
                        if endq:
                            accv = PS[bacc][:, 0:390].rearrange("p (h d) -> p h d", h=6)
                            CP("dve", den[:, :], accv[:, :, 64], [], [("ps", bacc), "den"])
                            RECIP(rden[:, :], den[:, :], ["den"], ["rden"])
                            for hh in range(6):
                                TSC("dve", osb[:, hh, :], accv[:, hh, 0:64], rden[:, hh:hh + 1], None, ALU.mult, None, ["rden"], [("ps", bacc), "osb"])
                            bt = bank()
                            for jc2 in range(3):
                                TR(PS[bt][:, jc2 * 128:(jc2 + 1) * 128], osb[:, 2 * jc2:2 * jc2 + 2, :].rearrange("p h d -> p (h d)"), ident[:, :], ["osb", "ident"], [("ps", bt)])
                            CP("act", mixT[:, 0:3, tq * 128:(tq + 1) * 128], PS[bt][:, 0:384].rearrange("p (k n) -> p k n", k=3), [], [("ps", bt), "mixA"])
                P.flush(side, len(side))
                dense_tail(l, G)
            final_out(G, lambda r0, nt: y_p[j * C + r0:j * C + r0 + nt, :])

        def sample_pass():
            G = [(0, NS)]
            v4 = lambda ap: ap.rearrange("p (a t) -> p a t", a=4)
            MEMSET(V1n[:, :, :, 64:65], 1.0, ["V1n"])
            load_x(x_s[:, :], NS, 0)
            P.dma("sp", cos_sb[:, 0:NS], cosS, writes=["cos"], tag="cos")
            P.dma("sp", sin_sb[:, 0:NS], sinS, writes=["sin"], tag="sin")
            for l in range(2):
                load_layer_consts(l)
                rmsnorm(lambda kc: VC(45 * l + kc), G, lambda kc, c0, n: hnT[:, kc, c0:c0 + n], "hnT")
                for b4 in range(4):
                    for k in range(2):
                        P.dma("sp", u_s[:, k, b4, 0:15], st_pool[l, b4, :, k * 128:(k + 1) * 128].rearrange("r p -> p r"), writes=["u"], tag="su")
                    for k in range(3):
                        P.dma("sp", xr_s[:, k, b4, 0:3], st_conv[l, b4, :, k * 128:(k + 1) * 128].rearrange("r p -> p r"), writes=["xr"], tag="sx")
                for k in range(3):
                    P.dma("sp", hs0[:, k, :], st_h[l, :, k * 128:(k + 1) * 128].rearrange("b p -> p b"), writes=["hs0"], tag="sh")

                def kf32_hook(jc, c0, n, ta, tb, ka, kb):
                    CP("dve", kfs[:, jc, :], ta[:, 0:NS], [ka], ["kfs"])

                w_in_feature(l, G, cosS, sinS, qTs, kTs, 0,
                             lambda kc, c0, n: u_s[:, kc, :, 15:23],
                             lambda kc, c0, n: xr_s[:, kc, :, 3:11], v4, kf32_hook, g_s)
                b = bank()
                for jc in range(3):
                    TR(PS[b][0:NS, jc * 128:(jc + 1) * 128], kfs[:, jc, :], ident[:, :], ["kfs", "ident"], [("ps", b)])
                CP("dve", vst[0:NS, :], PS[b][0:NS, 0:384], [], [("ps", b), "vst"])
                for b4 in range(4):
                    P.dma("sp", s_k[l, b4, 2040:2048, :], vst[b4 * 8:(b4 + 1) * 8, :], reads=["vst"], tag="skn")
                b = bank()
                for kc in range(8):
                    MM(PS[b][0:NS, 0:384], hnT[:, kc, 0:NS], Wv[:, kc, :], kc == 0, kc == 7, ["hnT", "Wv"], [("ps", b)])
                CP("act", ytok[0:NS, 0:384], PS[b][0:NS, 0:384], [], [("ps", b), "ytok"])
                for b4 in range(4):
                    P.dma("sp", s_v[l, b4, 2040:2048, :], ytok[b4 * 8:(b4 + 1) * 8, 0:384], reads=["ytok"], tag="svn")
                for b4 in range(4):
                    b = bank()
                    for kc in range(8):
                        MM(PS[b][0:8, 0:384], hnT[:, kc, b4 * 8:(b4 + 1) * 8], Wv[:, kc, :], kc == 0, kc == 7, ["hnT", "Wv"], [("ps", b)])
                    CP("dve", V1n[0:8, b4, :, 0:64], PS[b][0:8, 0:384].rearrange("p (h d) -> p h d", h=6), [], [("ps", b), "V1n"])
                MEMSET(V1c[:, :, :, 64:65], 1.0, ["V1c"])
                for b4 in range(4):
                    P.dma("sp", stage, ck[l, b4].rearrange("(t p) f -> p t f", p=128), writes=["stage"], tag="kc")
                    for t in range(16):
                        b = bank()
                        for jc in range(3):
                            TR(PS[b][:, jc * 128:(jc + 1) * 128], stage[:, t, jc * 128:(jc + 1) * 128], ident[:, :], ["stage", "ident"], [("ps", b)])
                        CP(ev_eng(), kTc[:, :, t * 128:(t + 1) * 128], PS[b][:, 0:384].rearrange("p (k n) -> p k n", k=3), [], [("ps", b), "kTc"])
                    P.dma("sp", stage, cv[l, b4].rearrange("(t p) f -> p t f", p=128), writes=["stage"], tag="kc")
                    CP("pool", V1c[:, :, :, 0:64], stage.rearrange("p t (h d) -> p t h d", h=6), ["stage"], ["V1c"])
                    bacc = acc_bank()
                    qcols = slice(b4 * 8, (b4 + 1) * 8)
                    for h in range(6):
                        jc, hp = h // 2, 64 * (h % 2)
                        b = bank()
                        for t in range(16):
                            MM(PS[b][:, t * 8:(t + 1) * 8], kTc[hp:hp + 64, jc, t * 128:(t + 1) * 128], qTs[hp:hp + 64, jc, qcols], True, True, ["kTc", "qT"], [("ps", b)])
                        MM(PS[b][0:8, 128:136], kTs[hp:hp + 64, jc, qcols], qTs[hp:hp + 64, jc, qcols], True, True, ["kT", "qT"], [("ps", b)])
                        ei = state["es"]; state["es"] = (ei + 1) % 5
                        ACT(es[ei][:, 0:128], PS[b][:, 0:128], AF.Exp, [], [("ps", b), ("es", ei)], scale=0.125)
                        ACT(es[ei][0:8, 128:136], PS[b][0:8, 128:136], AF.Exp, [], [("ps", b), ("es", ei)], scale=0.125)
                        TT("pool", es[ei][:, 0:128], es[ei][:, 0:128], mS[:, 0:128], ALU.mult, ["mS"], [("es", ei)])
                        TT("pool", es[ei][0:8, 128:136], es[ei][0:8, 128:136], mS[0:8, 128:136], ALU.mult, ["mS"], [("es", ei)])
                        for t in range(16):
                            MM(PS[bacc][0:8, h * 65:(h + 1) * 65], es[ei][:, t * 8:(t + 1) * 8], V1c[:, t, h, :], t == 0, False, [("es", ei), "V1c"], [("ps", bacc)])
                        MM(PS[bacc][0:8, h * 65:(h + 1) * 65], es[ei][0:8, 128:136], V1n[0:8, b4, h, :], False, True, [("es", ei), "V1n"], [("ps", bacc)])
                    accv = PS[bacc][0:8, 0:390].rearrange("p (h d) -> p h d", h=6)
                    CP("dve", den[0:8, :], accv[:, :, 64], [], [("ps", bacc), "den"])
                    RECIP(rden[0:8, :], den[0:8, :], ["den"], ["rden"])
                    for h in range(6):
                        TSC("dve", osb[0:8, h, :], accv[:, h, 0:64], rden[0:8, h:h + 1], None, ALU.mult, None, ["rden"], [("ps", bacc), "osb"])
                    bt = bank()
                    for jc in range(3):
                        TR(PS[bt][:, jc * 8:(jc + 1) * 8], osb[0:8, 2 * jc:2 * jc + 2, :].rearrange("p h d -> p (h d)"), ident[0:8, 0:8], ["osb", "ident"], [("ps", bt)])
                    CP("act", mixT[:, 0:3, qcols], PS[bt][:, 0:24].rearrange("p (k n) -> p k n", k=3), [], [("ps", bt), "hnT"])
                for kc in range(2):
                    A, B = TSP[0], TSP[1]
                    Pd = TS[0]
                    uk = u_s[:, kc, :, :]
                    TT("pool", A[:, :, 1:23], uk[:, :, 1:23], uk[:, :, 0:22], ALU.add, ["u"], ["pA"])
                    if kc == 0:
                        TT("pool", B[64:128, :, 3:23], A[64:128, :, 3:23], A[64:128, :, 1:21], ALU.add, ["pA"], ["pB"])
                        CP("pool", B[0:64, :, 15:23], A[0:64, :, 15:23], ["pA"], ["pB"])
                    else:
                        TT("pool", B[:, :, 3:23], A[:, :, 3:23], A[:, :, 1:21], ALU.add, ["pA"], ["pB"])
                        TT("pool", A[:, :, 7:23], B[:, :, 7:23], B[:, :, 3:19], ALU.add, ["pB"], ["pA"])
                        TT("pool", B[64:128, :, 15:23], A[64:128, :, 15:23], A[64:128, :, 7:15], ALU.add, ["pA"], ["pB"])
                        CP("pool", B[0:64, :, 15:23], A[0:64, :, 15:23], ["pA"], ["pB"])
                    STT("dve", v4(Pd[:, :]), B[:, :, 15:23], INVW(kc), uk[:, :, 15:23], ALU.mult, ALU.subtract, ["pB", "u", "vec"], ["rg0"])
                    b = bank()
                    MM(PS[b][:, 0:NS], BD[:, kc, :], Pd[:, :], True, True, ["rg0", "BD"], [("ps", b)])
                    TSC("dve", mixT[:, 3 + kc, 0:NS], PS[b][:, 0:NS], VC(45 * l + 16 + kc), None, ALU.mult, None, ["vec"], [("ps", b), "hnT"])
                for b4 in range(4):
                    for k in range(2):
                        P.dma("sp", s_pool[l, b4, :, k * 128:(k + 1) * 128].rearrange("r p -> p r"), u_s[:, k, b4, 8:23], reads=["u"], tag="spo")
                    for k in range(3):
                        P.dma("sp", s_conv[l, b4, :, k * 128:(k + 1) * 128].rearrange("r p -> p r"), xr_s[:, k, b4, 8:11], reads=["xr"], tag="sco")
                for kc in range(3):
                    tmp = tuple(v4(TS[i][:, :]) for i in range(4))
                    xk = xr_s[:, kc, :, :]

                    def scan_fn(a_, b_, h_, l=l, kc=kc):
                        for b4 in range(4):
                            SCAN(h_[:, b4, :], a_[:, b4, :], b_[:, b4, :], hs0[:, kc, b4:b4 + 1], ["rg1", "rg2", "hs0", "rg3"], ["rg3"])
                        CP("dve", hs0[:, kc, :], h_[:, :, 7], ["rg3"], ["hs0"])
                        TT("dve", v4(mixT[:, 5 + kc, 0:NS]), h_, v4(g_s[:, kc, 0:NS]), ALU.mult, ["rg3", "g"], ["hnT"])

                    mmv = [((lambda ap: ap.rearrange("p a t -> p (a t)")), NS)]
                    rglru_core(l, kc, xk[:, :, 3:11], xk[:, :, 2:10], xk[:, :, 1:9], xk[:, :, 0:8], tmp, mmv, scan_fn)
                for k in range(3):
                    P.dma("sp", s_h[l, :, k * 128:(k + 1) * 128].rearrange("b p -> p b"), hs0[:, k, :], reads=["hs0"], tag="sho")
                dense_tail(l, G)
            final_out(G, lambda r0, nt: y_s[r0:r0 + nt, :])

        convert_weights()
        for j in range(nch):
            plan_layer(0); plan_layer(1)
        if do_sample:
            plan_layer(0); plan_layer(1)
        for j in range(nch):
            prompt_chunk(j)
        for l in range(2):
            for k in range(2):
                P.dma("sp", p_pool[l, :, k * 128:(k + 1) * 128].rearrange("r p -> p r"), utail[:, l, k, :], reads=["utail"], tag="ppo")
            for k in range(3):
                P.dma("sp", p_conv[l, :, k * 128:(k + 1) * 128].rearrange("r p -> p r"), xrtail[:, l, k, :], reads=["xrtail"], tag="pco")
                P.dma("sp", p_h[l:l + 1, k * 128:(k + 1) * 128].rearrange("o p -> p o"), hst[:, l, k:k + 1], reads=["hst"], tag="pho")
        if do_sample:
            P.barrier()
            sample_pass()
        P.final_wait("sp")
        P.emit()
    return nc


def _prep_common(inp):
    f = lambda a: np.ascontiguousarray(np.asarray(a, dtype=np.float32))
    w_in = f(inp["w_in"])
    sw = np.arange(384).reshape(6, 2, 32)[:, ::-1, :].reshape(-1)
    wins, wvs, wouts, wgus, wdns, bds = [], [], [], [], [], []
    vec = np.zeros((128, 128), np.float32)
    for l in range(2):
        W = w_in[l]
        q, k, v = W[:, 0:384], W[:, 384:768], W[:, 768:1152]
        u, xr, gt = W[:, 1152:1408], W[:, 1408:1792], W[:, 1792:2176]
        ext = np.concatenate([q, k, q[:, sw], k[:, sw], u, xr, gt], 1)
        wins.append(_chunkmajor(ext, 8))
        wvs.append(np.ascontiguousarray(v.reshape(8, 128, 384).transpose(1, 0, 2).reshape(128, 8 * 384)))
        wouts.append(_chunkmajor(f(inp["w_out"])[l], 8))
        wgus.append(_chunkmajor(f(inp["w_gu"])[l], 8))
        wdns.append(_chunkmajor(f(inp["w_down"])[l], 22))
        pw, ga, gx = f(inp["pool_w"])[l], f(inp["gate_a_w"])[l], f(inp["gate_x_w"])[l]
        bds.append(np.stack([_blockdiag(pw[0:2]), _blockdiag(pw[2:4])] + [_blockdiag(ga[2 * i:2 * i + 2]) for i in range(3)]
                            + [_blockdiag(gx[2 * i:2 * i + 2]) for i in range(3)], 0))
        o = 45 * l
        vec[:, o:o + 8] = _vec128(f(inp["norm1_g"])[l]); vec[:, o + 8:o + 16] = _vec128(f(inp["norm2_g"])[l])
        vec[:, o + 16:o + 18] = _vec128(f(inp["pool_scale"])[l])
        cw = f(inp["conv_w"])[l]
        for j in range(4):
            vec[:, o + 18 + 3 * j:o + 21 + 3 * j] = _vec128(cw[j])
        vec[:, o + 30:o + 33] = _vec128(f(inp["conv_b"])[l]); vec[:, o + 33:o + 36] = _vec128(f(inp["gate_a_b"])[l])
        vec[:, o + 36:o + 39] = _vec128(f(inp["gate_x_b"])[l]); vec[:, o + 39:o + 42] = _vec128(f(inp["lru_lambda"])[l])
    vec[:, 90:98] = _vec128(f(inp["final_g"]))
    vec[0:64, 98] = 1 / 2; vec[64:128, 98] = 1 / 4; vec[0:64, 99] = 1 / 8; vec[64:128, 99] = 1 / 16
    cosT = np.zeros((NCH, 128, C), np.float32); sinT = np.zeros((NCH, 128, C), np.float32)
    for j in range(NCH):
        cosT[j], sinT[j] = _rope_tables(np.arange(j * C, (j + 1) * C))
    cs, ss = _rope_tables(16384 + np.arange(8))
    cosS = np.tile(cs, (1, 4)); sinS = np.tile(ss, (1, 4))
    p = np.arange(128)
    maskP = np.zeros((128, 17, 128), np.float32)
    for o in range(17):
        maskP[:, o, :] = _mult(128 * o + p[None, :] - p[:, None])
    maskS = np.zeros((128, 17, 8), np.float32)
    for t in range(16):
        maskS[:, t, :] = _mult(2048 + np.arange(8)[None, :] - (128 * t + p[:, None]))
    maskS[0:8, 16, :] = _mult(np.arange(8)[None, :] - np.arange(8)[:, None])
    invcnt = np.zeros((128, 2, 16), np.float32)
    wpp = np.array([[2] * 64 + [4] * 64, [8] * 64 + [16] * 64], np.float32)
    for kc in range(2):
        invcnt[:, kc, :] = 1.0 / np.minimum(np.arange(16)[None, :] + 1, wpp[kc][:, None])
    return dict(w_in=np.stack(wins), w_v=np.stack(wvs), w_out=np.stack(wouts), w_gu=np.stack(wgus), w_dn=np.stack(wdns),
                vecs=vec, bd=np.stack(bds), cosT=cosT, sinT=sinT, cosS=np.ascontiguousarray(cosS), sinS=np.ascontiguousarray(sinS),
                maskP=np.ascontiguousarray(maskP.reshape(128, -1)), maskS=np.ascontiguousarray(maskS.reshape(128, -1)),
                ident=np.eye(128, dtype=np.float32), invcnt=np.ascontiguousarray(invcnt.reshape(128, 32)))


_NC_CACHE = {}


def kernel(**inp):
    f = lambda a: np.ascontiguousarray(np.asarray(a, dtype=np.float32))
    common = _prep_common(inp)
    common["x_p"] = f(inp["x_prompt"])[0]
    xs = f(inp["x_sample"]); ckk = f(inp["cache_win_k"]).reshape(2, 32, 2048, 384); cvv = f(inp["cache_win_v"]).reshape(2, 32, 2048, 384)
    sp_, sc_, sh_ = f(inp["state_pool"]), f(inp["state_conv"]), f(inp["state_rglru"])
    in_maps = []
    for c in range(NCORE):
        m = dict(common)
        sl = slice(4 * c, 4 * c + 4)
        m["x_s"] = np.ascontiguousarray(xs[sl].reshape(NS, D))
        m["ck"] = np.ascontiguousarray(ckk[:, sl]); m["cv"] = np.ascontiguousarray(cvv[:, sl])
        m["st_pool"] = np.ascontiguousarray(sp_[:, sl]); m["st_conv"] = np.ascontiguousarray(sc_[:, sl]); m["st_h"] = np.ascontiguousarray(sh_[:, sl])
        in_maps.append(m)
    if "nc" not in _NC_CACHE:
        _NC_CACHE["nc"] = build()
    res = run_bass_kernel_spmd(_NC_CACHE["nc"], in_maps, core_ids=list(range(NCORE)))
    R = res.results
    cat = lambda name, ax: np.concatenate([R[c][name] for c in range(NCORE)], axis=ax)
    y_prompt = R[0]["y_p"].reshape(1, SEQ, D)
    y_sample = cat("y_s", 0).reshape(32, 8, D)
    p_win_k = R[0]["p_k"].reshape(2, 1, 2048, 6, 64); p_win_v = R[0]["p_v"].reshape(2, 1, 2048, 6, 64)
    p_pool = R[0]["p_pool"].reshape(2, 1, 15, 256); p_conv = R[0]["p_conv"].reshape(2, 1, 3, 384); p_rglru = R[0]["p_h"].reshape(2, 1, 384)
    s_win_k = cat("s_k", 1).reshape(2, 32, 2048, 6, 64); s_win_v = cat("s_v", 1).reshape(2, 32, 2048, 6, 64)
    s_pool = cat("s_pool", 1); s_conv = cat("s_conv", 1); s_rglru = cat("s_h", 1)
    return tuple(np.ascontiguousarray(a.astype(np.float32)) for a in
                 (y_prompt, y_sample, p_win_k, p_win_v, p_pool, p_conv, p_rglru, s_win_k, s_win_v, s_pool, s_conv, s_rglru))
```
